# Optimizing a Trainium2 kernel written in Bass

```python
import math
import jax
import jax.numpy as jnp
from jax import lax
import numpy as np

D_MODEL = 1024
BATCH = 8
SEQ = 4096
DEPTH = 4

A_HEADS = 8
A_KV_GROUPS = 2
A_HEAD_DIM = 64
CMP_BLOCK = 32
CMP_STRIDE = 16
SLC_BLOCK = 64
SLC_TOPK = 16
N_LOCAL_BLOCKS = 2
WINDOW = 512
NSA_Q_BLOCK = 64
FORCE_SCORE = 1e9
B_HEADS = 8
Q_LORA = 256
KV_LORA = 128
NOPE_DIM = 64
ROPE_DIM = 32
V_DIM = 64
ROPE_THETA = 10000.0
Q_BLOCK = 128
C_HEADS = 8
C_HEAD_DIM = 64
C_WIDTH = C_HEADS * C_HEAD_DIM
DECAY_LORA = 64
AAA_LORA = 64
GATE_LORA = 128
GN_EPS = 64e-5
N_BUCKETS = 32
MAX_DISTANCE = 128
D_FF = 2816
CONV_WIDTH = 3
EPS = 1e-6

A_Q = A_HEADS * A_HEAD_DIM
A_KV = A_KV_GROUPS * A_HEAD_DIM
A_GATE = 3 * A_HEADS
A_COLS = A_Q + 6 * A_KV + A_GATE
B_COLS = Q_LORA + KV_LORA + ROPE_DIM
C_COLS = 3 * C_WIDTH + DECAY_LORA + AAA_LORA + GATE_LORA
MERGE_COLS = 3 * D_MODEL
IN_COLS = A_COLS + B_COLS + C_COLS + MERGE_COLS
A_OUT = A_HEADS * A_HEAD_DIM
B_OUT = B_HEADS * V_DIM
C_OUT = C_WIDTH
MIX_WIDTH = A_OUT + B_OUT + C_OUT

kernel_name = 'hybrid_nsa_mla_rwkv7_gated_merge'


def _split(x, sizes):
    return jnp.split(x, np.cumsum(sizes)[:-1].tolist(), axis=-1)


def rms_norm(x, gain):
    xf = x.astype(jnp.float32)
    y = xf * lax.rsqrt(jnp.mean(xf * xf, axis=-1, keepdims=True) + EPS)
    return (y * gain.astype(jnp.float32)).astype(x.dtype)


def masked_softmax(logits, mask):
    logits = jnp.where(mask, logits.astype(jnp.float32), -1e30)
    m = jnp.max(logits, axis=-1, keepdims=True)
    p = jnp.where(mask, jnp.exp(logits - m), 0.0)
    return p / jnp.maximum(jnp.sum(p, axis=-1, keepdims=True), 1e-30)


def t5_bucket(dist):
    max_exact = N_BUCKETS // 2
    d = jnp.maximum(dist, 0)
    large = max_exact + (jnp.log(jnp.maximum(d, 1).astype(jnp.float32) / max_exact)
                         / math.log(MAX_DISTANCE / max_exact) * (N_BUCKETS - max_exact)).astype(jnp.int32)
    return jnp.where(d < max_exact, d, jnp.minimum(large, N_BUCKETS - 1))


def shift_right(x):
    return jnp.pad(x, ((0, 0), (1, 0), (0, 0)))[:, :x.shape[1]]


def causal_dwconv(x, w, b):
    K, C = w.shape
    y = lax.conv_general_dilated(x, w[:, None, :].astype(x.dtype), window_strides=(1,),
                                 padding=[(K - 1, 0)], dimension_numbers=('NWC', 'WIO', 'NWC'),
                                 feature_group_count=C)
    return y + b


def rope(x, pos):
    half = x.shape[-1] // 2
    inv = ROPE_THETA ** (-jnp.arange(half, dtype=jnp.float32) / half)
    ang = pos.astype(jnp.float32)[:, None] * inv
    cos, sin = jnp.cos(ang)[None, :, None, :], jnp.sin(ang)[None, :, None, :]
    x1, x2 = x[..., :half], x[..., half:]
    return jnp.concatenate([x1 * cos - x2 * sin, x1 * sin + x2 * cos], axis=-1).astype(x.dtype)


def nsa_attention(q, k_cmp, v_cmp, k_slc, v_slc, k_win, v_win, gates, cmp_pos, cmp_w1, cmp_w2, rel_bias):
    B, S = q.shape[:2]
    G, HPG, Dh = A_KV_GROUPS, A_HEADS // A_KV_GROUPS, A_HEAD_DIM
    Qb = NSA_Q_BLOCK
    scale = Dh ** -0.5
    n_cmp = (S - CMP_BLOCK) // CMP_STRIDE + 1
    blk = np.arange(n_cmp)[:, None] * CMP_STRIDE + np.arange(CMP_BLOCK)[None, :]

    def compress(t, j):
        tb = t[:, blk] + cmp_pos[j][None, None, :, None, :]
        tb = tb.transpose(0, 1, 3, 2, 4).reshape(B, n_cmp, G, CMP_BLOCK * Dh)
        return jax.nn.gelu(tb @ cmp_w1[j]) @ cmp_w2[j]

    kc, vc = compress(k_cmp, 0), compress(v_cmp, 1)
    cmp_end = jnp.arange(n_cmp) * CMP_STRIDE + CMP_BLOCK - 1
    n_slc = S // SLC_BLOCK
    r_s, r_c = SLC_BLOCK // CMP_STRIDE, CMP_BLOCK // CMP_STRIDE
    cidx = (r_s * np.arange(n_slc)[:, None, None] - np.arange(r_s)[None, :, None]
            - np.arange(r_c)[None, None, :]).reshape(n_slc, -1)
    M = jnp.asarray((cidx[:, :, None] == np.arange(n_cmp)[None, None, :]).sum(1).T.astype(np.float32))
    k_sel = min(SLC_TOPK, n_slc)
    ks_blocks = k_slc.reshape(B, n_slc, SLC_BLOCK, G, Dh).transpose(0, 3, 1, 2, 4)
    vs_blocks = v_slc.reshape(B, n_slc, SLC_BLOCK, G, Dh).transpose(0, 3, 1, 2, 4)
    kw_pad = jnp.pad(k_win, ((0, 0), (WINDOW, 0), (0, 0), (0, 0)))
    vw_pad = jnp.pad(v_win, ((0, 0), (WINDOW, 0), (0, 0), (0, 0)))
    rel_g = rel_bias.reshape(N_BUCKETS, G, HPG)
    b_ar = jnp.arange(B)[:, None, None, None]
    g_ar = jnp.arange(G)[None, :, None, None]
    blk_id = jnp.arange(n_slc)

    def bias_hq(dist):
        return rel_bias[t5_bucket(dist)].transpose(2, 0, 1).reshape(G, HPG, *dist.shape)

    def block(i):
        start = i * Qb
        qb = lax.dynamic_slice_in_dim(q, start, Qb, 1).reshape(B, Qb, G, HPG, Dh)
        gb = lax.dynamic_slice_in_dim(gates, start, Qb, 1).reshape(B, Qb, G, HPG, 3)
        t = start + jnp.arange(Qb)
        dist_c = t[:, None] - cmp_end[None, :]
        s_c = jnp.einsum('bqghd,bngd->bghqn', qb, kc) * scale + bias_hq(dist_c)
        p_c = masked_softmax(s_c, dist_c >= 0)
        o_c = jnp.einsum('bghqn,bngd->bqghd', p_c.astype(vc.dtype), vc)
        imp = jnp.einsum('bghqn,nj->bgqj', p_c, M)
        cur = t // SLC_BLOCK
        back = cur[:, None] - blk_id[None, :]
        forced = (blk_id[None, :] == 0) | ((back >= 0) & (back < N_LOCAL_BLOCKS))
        score = jnp.where(forced, FORCE_SCORE, jnp.where(back >= 0, imp, -1.0))
        _, idx = lax.top_k(score, k_sel)
        ks = ks_blocks[b_ar, g_ar, idx].reshape(B, G, Qb, k_sel * SLC_BLOCK, Dh)
        vs = vs_blocks[b_ar, g_ar, idx].reshape(B, G, Qb, k_sel * SLC_BLOCK, Dh)
        kpos = (idx[..., None] * SLC_BLOCK + jnp.arange(SLC_BLOCK)).reshape(B, G, Qb, k_sel * SLC_BLOCK)
        dist_s = t[None, None, :, None] - kpos
        bias_s = rel_g[t5_bucket(dist_s), g_ar].transpose(0, 1, 4, 2, 3)
        s_s = jnp.einsum('bqghd,bgqkd->bghqk', qb, ks) * scale + bias_s
        p_s = masked_softmax(s_s, (dist_s >= 0)[:, :, None])
        o_s = jnp.einsum('bghqk,bgqkd->bqghd', p_s.astype(vs.dtype), vs)
        kw = lax.dynamic_slice_in_dim(kw_pad, start, Qb + WINDOW, 1)
        vw = lax.dynamic_slice_in_dim(vw_pad, start, Qb + WINDOW, 1)
        wpos = start - WINDOW + jnp.arange(Qb + WINDOW)
        dist_w = t[:, None] - wpos[None, :]
        mask_w = (dist_w >= 0) & (dist_w < WINDOW) & (wpos[None, :] >= 0)
        s_w = jnp.einsum('bqghd,bkgd->bghqk', qb, kw) * scale + bias_hq(dist_w)
        p_w = masked_softmax(s_w, mask_w)
        o_w = jnp.einsum('bghqk,bkgd->bqghd', p_w.astype(vw.dtype), vw)
        return gb[..., 0:1] * o_c + gb[..., 1:2] * o_s + gb[..., 2:3] * o_w

    out = lax.map(block, jnp.arange(S // Qb))
    return out.transpose(1, 0, 2, 3, 4, 5).reshape(B, S, A_HEADS * Dh)


def causal_attention(q, k, v):
    B, S, H, Dqk = q.shape
    scale = Dqk ** -0.5
    kpos = jnp.arange(S)

    def block(i):
        qb = lax.dynamic_slice_in_dim(q, i * Q_BLOCK, Q_BLOCK, 1)
        s = jnp.einsum('bqhd,bkhd->bhqk', qb, k) * scale
        qpos = i * Q_BLOCK + jnp.arange(Q_BLOCK)
        p = masked_softmax(s, kpos[None, :] <= qpos[:, None])
        return jnp.einsum('bhqk,bkhd->bqhd', p.astype(v.dtype), v)

    out = lax.map(block, jnp.arange(S // Q_BLOCK))
    return out.transpose(1, 0, 2, 3, 4).reshape(B, S, H * v.shape[-1])


def mla_branch(qa, kva, kr, q_norm, kv_norm, w_uq, w_ukv):
    B, S = qa.shape[:2]
    pos = jnp.arange(S)
    q = (rms_norm(qa, q_norm) @ w_uq).reshape(B, S, B_HEADS, NOPE_DIM + ROPE_DIM)
    kv = (rms_norm(kva, kv_norm) @ w_ukv).reshape(B, S, B_HEADS, NOPE_DIM + V_DIM)
    q = jnp.concatenate([q[..., :NOPE_DIM], rope(q[..., NOPE_DIM:], pos)], axis=-1)
    k_rope = jnp.broadcast_to(rope(kr[:, :, None, :], pos), (B, S, B_HEADS, ROPE_DIM))
    k = jnp.concatenate([kv[..., :NOPE_DIM], k_rope], axis=-1)
    return causal_attention(q, k, kv[..., NOPE_DIM:])


def rwkv7_branch(p, mu, w0, a0, k_k, k_a, w2, a2, g2, r_k, ln_x):
    B, S = p.shape[:2]
    p = p + (shift_right(p) - p) * mu
    r, k, v, wd, ad, gd = _split(p, [C_WIDTH] * 3 + [DECAY_LORA, AAA_LORA, GATE_LORA])
    w = w0 + jnp.tanh(wd) @ w2
    decay = jnp.exp(-jnp.exp(-jax.nn.softplus(-w) - 0.5))
    a = jax.nn.sigmoid(a0 + ad @ a2)
    g = jax.nn.sigmoid(gd) @ g2
    heads = lambda t: t.reshape(B, S, C_HEADS, C_HEAD_DIM)
    kk = heads(k * k_k)
    kk = kk / jnp.maximum(jnp.linalg.norm(kk, axis=-1, keepdims=True), 1e-12)
    k = k * (1.0 + (a - 1.0) * k_a)
    r_h, k_h, v_h, a_h, d_h = heads(r), heads(k), heads(v), heads(a), heads(decay)

    def step(state, inp):
        r_t, d_t, k_t, v_t, kk_t, a_t = inp
        sa = jnp.einsum('bhij,bhj->bhi', state, -kk_t)
        state = (state * d_t[:, :, None, :] + sa[..., None] * (kk_t * a_t)[:, :, None, :]
                 + v_t[..., None] * k_t[:, :, None, :])
        return state, jnp.einsum('bhij,bhj->bhi', state, r_t)

    xs = tuple(jnp.moveaxis(t.astype(jnp.float32), 1, 0) for t in (r_h, d_h, k_h, v_h, kk, a_h))
    state0 = jnp.zeros((B, C_HEADS, C_HEAD_DIM, C_HEAD_DIM), jnp.float32)
    _, y = lax.scan(step, state0, xs)
    y = jnp.moveaxis(y, 0, 1)
    mean = jnp.mean(y, axis=-1, keepdims=True)
    var = jnp.mean(jnp.square(y - mean), axis=-1, keepdims=True)
    y = ((y - mean) * lax.rsqrt(var + GN_EPS)).reshape(B, S, C_WIDTH) * ln_x[0] + ln_x[1]
    bonus = (jnp.sum(r_h * k_h * r_k, axis=-1, keepdims=True) * v_h).reshape(B, S, C_WIDTH)
    return (y.astype(p.dtype) + bonus) * g


def token_mixer(h, w_in, cmp_pos, cmp_w1, cmp_w2, rel_bias, q_norm, kv_norm, w_uq, w_ukv,
                mu, w0, a0, k_k, k_a, w2, a2, g2, r_k, ln_x, w_branch, w_out):
    B, S, _ = h.shape
    z = h @ w_in
    za, zb, zc, zg = _split(z, [A_COLS, B_COLS, C_COLS, MERGE_COLS])
    aq, akc, avc, aks, avs, akw, avw, agate = _split(za, [A_Q] + [A_KV] * 6 + [A_GATE])
    grp = lambda t: t.reshape(B, S, A_KV_GROUPS, A_HEAD_DIM)
    y_a = nsa_attention(aq.reshape(B, S, A_HEADS, A_HEAD_DIM), grp(akc), grp(avc), grp(aks), grp(avs),
                        grp(akw), grp(avw), jax.nn.sigmoid(agate).reshape(B, S, A_HEADS, 3),
                        cmp_pos, cmp_w1, cmp_w2, rel_bias)
    qa, kva, kr = _split(zb, [Q_LORA, KV_LORA, ROPE_DIM])
    y_b = mla_branch(qa, kva, kr, q_norm, kv_norm, w_uq, w_ukv)
    y_c = rwkv7_branch(zc, mu, w0, a0, k_k, k_a, w2, a2, g2, r_k, ln_x)
    g_a, g_b, g_c = jnp.split(jax.nn.sigmoid(zg), 3, axis=-1)
    w_a, w_b, w_c = w_branch[:A_OUT], w_branch[A_OUT:A_OUT + B_OUT], w_branch[A_OUT + B_OUT:]
    merged = g_a * (y_a @ w_a) + g_b * (y_b @ w_b) + g_c * (y_c @ w_c)
    return merged @ w_out


def conv_ffn(h, w_up, conv_w, conv_b, w_down):
    u = causal_dwconv(h @ w_up, conv_w, conv_b)
    gate, val = jnp.split(u, 2, axis=-1)
    return (jax.nn.gelu(gate) * val) @ w_down


def setup_inputs(seed: int = 0) -> dict:
    key = jax.random.key(seed)
    keys = iter(jax.random.split(key, 40))
    L, D = DEPTH, D_MODEL

    def nrm(shape, scale):
        return jax.random.normal(next(keys), shape, jnp.float32) * scale

    return {
        'x': nrm((BATCH, SEQ, D), 1.0),
        'c': nrm((BATCH, D), 1.0),
        'rel_bias': nrm((N_BUCKETS, A_HEADS), 0.5),
        'ada_w': nrm((L, D, 6 * D), 0.5 * D ** -0.5),
        'ada_b': nrm((L, 6 * D), 0.02),
        'norm_gain': 1.0 + nrm((L, 4, D), 0.05),
        'w_in': nrm((L, D, IN_COLS), D ** -0.5),
        'nsa_cmp_pos': nrm((L, 2, CMP_BLOCK, A_HEAD_DIM), 0.1),
        'nsa_cmp_w1': nrm((L, 2, CMP_BLOCK * A_HEAD_DIM, A_HEAD_DIM), (CMP_BLOCK * A_HEAD_DIM) ** -0.5),
        'nsa_cmp_w2': nrm((L, 2, A_HEAD_DIM, A_HEAD_DIM), A_HEAD_DIM ** -0.5),
        'mla_q_norm': 1.0 + nrm((L, Q_LORA), 0.05),
        'mla_kv_norm': 1.0 + nrm((L, KV_LORA), 0.05),
        'mla_w_uq': nrm((L, Q_LORA, B_HEADS * (NOPE_DIM + ROPE_DIM)), Q_LORA ** -0.5),
        'mla_w_ukv': nrm((L, KV_LORA, B_HEADS * (NOPE_DIM + V_DIM)), KV_LORA ** -0.5),
        'rwkv_mu': jax.random.uniform(next(keys), (L, C_COLS), jnp.float32),
        'rwkv_w0': jax.random.uniform(next(keys), (L, C_WIDTH), jnp.float32, -6.0, 1.0),
        'rwkv_a0': nrm((L, C_WIDTH), 0.1),
        'rwkv_k_k': 0.85 + nrm((L, C_WIDTH), 0.05),
        'rwkv_k_a': 1.0 + nrm((L, C_WIDTH), 0.05),
        'rwkv_w2': nrm((L, DECAY_LORA, C_WIDTH), DECAY_LORA ** -0.5),
        'rwkv_a2': nrm((L, AAA_LORA, C_WIDTH), AAA_LORA ** -0.5),
        'rwkv_g2': nrm((L, GATE_LORA, C_WIDTH), GATE_LORA ** -0.5),
        'rwkv_r_k': nrm((L, C_HEADS, C_HEAD_DIM), 0.1),
        'rwkv_ln': jnp.stack([1.0 + nrm((L, C_WIDTH), 0.05), nrm((L, C_WIDTH), 0.02)], axis=1),
        'w_branch': nrm((L, MIX_WIDTH, D), A_OUT ** -0.5),
        'w_out': nrm((L, D, D), D ** -0.5),
        'ffn_up': nrm((L, D, 2 * D_FF), D ** -0.5),
        'ffn_conv_w': nrm((L, CONV_WIDTH, 2 * D_FF), CONV_WIDTH ** -0.5),
        'ffn_conv_b': nrm((L, 2 * D_FF), 0.02),
        'ffn_down': nrm((L, D_FF, D), D_FF ** -0.5),
    }


def reference(x, c, rel_bias, ada_w, ada_b, norm_gain, w_in, nsa_cmp_pos, nsa_cmp_w1, nsa_cmp_w2,
              mla_q_norm, mla_kv_norm, mla_w_uq, mla_w_ukv, rwkv_mu, rwkv_w0, rwkv_a0, rwkv_k_k,
              rwkv_k_a, rwkv_w2, rwkv_a2, rwkv_g2, rwkv_r_k, rwkv_ln, w_branch, w_out,
              ffn_up, ffn_conv_w, ffn_conv_b, ffn_down):
    cond = jax.nn.silu(c)
    for l in range(DEPTH):
        mod = (cond @ ada_w[l] + ada_b[l])[:, None, :]
        sh1, sc1, gt1, sh2, sc2, gt2 = jnp.split(mod, 6, axis=-1)
        h = rms_norm(x, norm_gain[l, 0]) * (1.0 + sc1) + sh1
        y = token_mixer(h, w_in[l], nsa_cmp_pos[l], nsa_cmp_w1[l], nsa_cmp_w2[l], rel_bias,
                        mla_q_norm[l], mla_kv_norm[l], mla_w_uq[l], mla_w_ukv[l],
                        rwkv_mu[l], rwkv_w0[l], rwkv_a0[l], rwkv_k_k[l], rwkv_k_a[l],
                        rwkv_w2[l], rwkv_a2[l], rwkv_g2[l], rwkv_r_k[l], rwkv_ln[l],
                        w_branch[l], w_out[l])
        x = x + gt1 * rms_norm(y, norm_gain[l, 1])
        h = rms_norm(x, norm_gain[l, 2]) * (1.0 + sc2) + sh2
        f = conv_ffn(h, ffn_up[l], ffn_conv_w[l], ffn_conv_b[l], ffn_down[l])
        x = x + gt2 * rms_norm(f, norm_gain[l, 3])
    return x
```

```python
import math
from contextlib import ExitStack
import numpy as np
import concourse.bass as bass
import concourse.mybir as mybir
from concourse.bass_utils import run_bass_kernel_spmd


F32 = mybir.dt.float32
BF16 = mybir.dt.bfloat16
I32 = mybir.dt.int32
AF = mybir.ActivationFunctionType
ALU = mybir.AluOpType
AX = mybir.AxisListType

ENG_BLOCK = {'pe': 'tensor', 'act': 'scalar', 'dve': 'vector', 'pool': 'gpsimd', 'sp': 'sync'}


class Buf:
    __slots__ = ('w', 'r', 'name', 'epoch')

    def __init__(self, name=''):
        self.w = {}
        self.r = {}
        self.name = name
        self.epoch = 0


class Tn:
    def __init__(self, t, name=''):
        self.t = t
        self.b = Buf(name)

    def __getitem__(self, k):
        return self.t[k]


class Prog:
    def __init__(self, nc, stack, ring=12, same_engine_sync=True):
        self.nc = nc
        self.stack = stack
        self.same = same_engine_sync
        self.names = list(ENG_BLOCK)
        self.insts = {k: [] for k in self.names}
        self.cnt = {k: 0 for k in self.names}
        self.semobj = {}
        self.semkey = {}
        for k in self.names:
            s = stack.enter_context(nc.semaphore('s_' + k))
            self.semobj['e_' + k] = s
            self.semkey[k] = 'e_' + k
        self.known = {k: {} for k in self.names}
        self.pending = {}
        self.epoch = 0
        self.nbar = 0
        for nm in ('bar0', 'bar1', 'clr0', 'clr1'):
            self.semobj[nm] = stack.enter_context(nc.semaphore(nm))
        self.rings = {}
        for q in ('sp', 'pool', 'act'):
            slots = []
            for i in range(ring):
                key = 'd_%s_%d' % (q, i)
                self.semobj[key] = stack.enter_context(nc.semaphore(key))
                slots.append([key, 0])
            self.rings[q] = [slots, 0]

    def _waits(self, en, reads, writes, extra=()):
        waits = {}

        def need(key, val):
            if waits.get(key, 0) < val:
                waits[key] = val

        for b in list(reads) + list(writes):
            if b.epoch != self.epoch:
                b.w = {}; b.r = {}; b.epoch = self.epoch
        for b in reads:
            for k, v in b.w.items():
                need(k, v)
        for b in writes:
            for k, v in b.w.items():
                need(k, v)
            for k, v in b.r.items():
                need(k, v)
        for k, v in extra:
            need(k, v)
        for k, v in self.pending.get(en, ()):
            need(k, v)
        self.pending[en] = []
        my = self.semkey[en]
        wl = []
        kn = self.known[en]
        for k, v in waits.items():
            if k == my and (en == 'pe' or not self.same):
                continue
            if kn.get(k, 0) >= v:
                continue
            kn[k] = v
            wl.append((k, v))
        return wl

    def _mark(self, tok, reads, writes):
        k, v = tok
        for b in writes:
            b.w = {k: v}
            b.r = {}
        for b in reads:
            if b.r.get(k, 0) < v:
                b.r[k] = v

    def barrier(self):
        extra = []
        for q, (slots, pos) in self.rings.items():
            for key, tot in slots:
                if tot > 0:
                    extra.append((key, tot))
        for k in self.names:
            if self.cnt[k] > 0:
                extra.append((self.semkey[k], self.cnt[k]))
        for k in self.names:
            self.pending[k] = list(extra)

    def _skip(self):
        lim = getattr(self, 'limit', None)
        if lim is None:
            return False
        self.nops = getattr(self, 'nops', 0) + 1
        return self.nops > lim

    def op(self, en, fn, reads=(), writes=()):
        if self._skip():
            return None
        reads = [getattr(b, 'b', b) for b in reads]
        writes = [getattr(b, 'b', b) for b in writes]
        wl = self._waits(en, reads, writes)
        self.cnt[en] += 1
        tok = (self.semkey[en], self.cnt[en])
        self.insts[en].append((wl, fn, None))
        self._mark(tok, reads, writes)
        return tok

    def dma(self, q, out, in_, reads=(), writes=(), **kw):
        if self._skip():
            return None
        reads = [getattr(b, 'b', b) for b in reads]
        writes = [getattr(b, 'b', b) for b in writes]
        slots, pos = self.rings[q]
        slot = slots[pos % len(slots)]
        self.rings[q][1] = pos + 1
        extra = [(slot[0], slot[1])] if slot[1] > 0 else []
        wl = self._waits(q, reads, writes, extra)
        slot[1] += 16
        tok = (slot[0], slot[1])
        self.insts[q].append((wl, (lambda e: e.dma_start(out=out, in_=in_, **kw)), slot[0]))
        self._mark(tok, reads, writes)
        return tok

    def finish(self):
        extra = []
        for q, (slots, pos) in self.rings.items():
            for key, tot in slots:
                if tot > 0:
                    extra.append((key, tot))
        for k in self.names:
            if self.cnt[k] > 0 and k != 'sp':
                extra.append((self.semkey[k], self.cnt[k]))
        wl = self._waits('sp', (), (), extra)
        self.insts['sp'].append((wl, None, None))

    def emit(self):
        nc = self.nc
        sig = {en: set() for en in self.names}
        for en in self.names:
            for wl, fn, kind in self.insts[en]:
                for k, v in wl:
                    if k.startswith('e_'):
                        sig[k[2:]].add(v)
        cmap = {en: {idx: i + 1 for i, idx in enumerate(sorted(sig[en]))} for en in self.names}
        self.sigcount = {en: len(sig[en]) for en in self.names}
        with nc.Block() as block:
            for en in self.names:
                insts = self.insts[en]

                def body(e, insts=insts, en=en):
                    mysem = self.semobj[self.semkey[en]]
                    opidx = 0
                    for wl, fn, kind in insts:
                        for k, v in wl:
                            if k.startswith('e_'):
                                v = cmap[k[2:]][v]
                            e.wait_ge(self.semobj[k], v)
                        if fn is None:
                            continue
                        inst = fn(e)
                        if kind is None:
                            opidx += 1
                            if opidx in sig[en]:
                                inst.then_inc(mysem, 1)
                        else:
                            inst.then_inc(self.semobj[kind], 16)

                getattr(block, ENG_BLOCK[en])(body)

    def stats(self):
        return {k: len(v) for k, v in self.insts.items()}


D = 1024; S = 4096; L = 4; NT = S // 128
IN_COLS = 6584; DFF = 2816
B0 = 1304; C0 = 1720; G0 = 3512
OFF = 4224; TABL = OFF + 4096 + 128
NEG = -30000.0
_uid = [0]


class Scope:
    def __init__(self, K):
        self.K = K; self.st = ExitStack()

    def __enter__(self):
        self.st.__enter__(); return self

    def __exit__(self, *a):
        self.K.P.barrier()
        return self.st.__exit__(*a)

    def sb(self, shape, dt=F32, n='t'):
        _uid[0] += 1
        return Tn(self.st.enter_context(self.K.nc.sbuf_tensor("%s_%d" % (n, _uid[0]), list(shape), dt)), n)

    def ps(self, shape, dt=F32, n='p'):
        _uid[0] += 1
        return Tn(self.st.enter_context(self.K.nc.psum_tensor("%s_%d" % (n, _uid[0]), list(shape), dt)), n)


def mm(P, out_tn, out_ap, lhsT, rhs, start, stop, reads):
    P.op('pe', lambda e: e.matmul(out_ap, lhsT=lhsT, rhs=rhs, start=start, stop=stop), reads=reads, writes=[out_tn])


def tr(P, out_tn, out_ap, in_ap, ident_ap, reads):
    if in_ap.dtype == F32:
        P.op('pe', lambda e: e.matmul(out_ap, lhsT=in_ap, rhs=ident_ap, start=True, stop=True), reads=reads, writes=[out_tn])
    else:
        P.op('pe', lambda e: e.transpose(out=out_ap, in_=in_ap, identity=ident_ap), reads=reads, writes=[out_tn])


def act(P, out_tn, out_ap, in_ap, func, reads, **kw):
    P.op('act', lambda e: e.activation(out=out_ap, in_=in_ap, func=func, **kw), reads=reads, writes=[out_tn] if not isinstance(out_tn, list) else out_tn)


def cp(P, en, out_tn, out_ap, in_ap, reads):
    if en == 'act':
        P.op('act', lambda e: e.copy(out=out_ap, in_=in_ap), reads=reads, writes=[out_tn])
    else:
        P.op(en, lambda e: e.tensor_copy(out=out_ap, in_=in_ap), reads=reads, writes=[out_tn])


def tt(P, en, out_tn, out_ap, a, b, op, reads):
    P.op(en, lambda e: e.tensor_tensor(out=out_ap, in0=a, in1=b, op=op), reads=reads, writes=[out_tn])


def ts(P, en, out_tn, out_ap, a, s1, s2, op0, op1, reads):
    if op1 is None:
        P.op(en, lambda e: e.tensor_scalar(out=out_ap, in0=a, scalar1=s1, scalar2=None, op0=op0), reads=reads, writes=[out_tn])
    else:
        P.op(en, lambda e: e.tensor_scalar(out=out_ap, in0=a, scalar1=s1, scalar2=s2, op0=op0, op1=op1), reads=reads, writes=[out_tn])


def stt(P, out_tn, out_ap, a, s, b, op0, op1, reads):
    P.op('dve', lambda e: e.scalar_tensor_tensor(out=out_ap, in0=a, scalar=s, in1=b, op0=op0, op1=op1), reads=reads, writes=[out_tn])


def phase_init(K, G):
    P = K.P
    G.ident_f = G.sb([128, 128], F32); G.ident_b = G.sb([128, 128], BF16)
    G.ones_f = G.sb([128, 128], F32); G.eps6 = G.sb([128, 1], F32); G.ones_b = G.sb([128, 128], BF16)
    P.dma('sp', G.ident_f[:], K.C['ident'], writes=[G.ident_f])
    cp(P, 'dve', G.ident_b, G.ident_b[:], G.ident_f[:], [G.ident_f])
    G.identR_b = G.sb([128, 128], BF16)
    load_w_cast(P, G.identR_b, G.identR_b[:], K.C['identR'])
    P.op('dve', lambda e: e.memset(G.ones_f[:], 1.0), writes=[G.ones_f])
    P.op('dve', lambda e: e.memset(G.ones_b[:], 1.0), writes=[G.ones_b])
    P.op('dve', lambda e: e.memset(G.eps6[:], 1e-6), writes=[G.eps6])
    G.condB = G.sb([128, 8, 128], F32)
    cT = G.sb([128, 8], F32); cs = G.sb([128, 8], F32)
    P.dma('sp', cT[:], K.c_in.rearrange("o (k p) -> p (o k)", p=128), writes=[cT], allow_slow_non_contiguous=True)
    act(P, cs, cs[:], cT[:], AF.Silu, [cT])
    cp(P, 'dve', G.condB, G.condB[:], cs[:].unsqueeze(2).broadcast_to([128, 8, 128]), [cs])
    G.A1, G.B1, G.A2, G.B2, G.G1, G.G2 = [G.sb([128, D], F32) for _ in range(6)]


def phase_ada(K, G, l):
    P = K.P
    with Scope(K) as sc:
        modB = sc.sb([128, 6 * D], F32); gains = sc.sb([128, 4 * D], F32)
        wb = [sc.sb([128, 8, 512], F32) for _ in range(2)]; bb = [sc.sb([1, 512], F32) for _ in range(2)]
        ps = [sc.ps([128, 512], F32) for _ in range(2)]
        P.dma('sp', gains[:], K.W['norm_gain'][l:l + 1].rearrange("o a d -> o (a d)").broadcast_to([128, 4 * D]), writes=[gains])
        for nb in range(12):
            w = wb[nb % 2]; b_ = bb[nb % 2]; p = ps[nb % 2]
            P.dma('sp', w[:], K.W['ada_w'][l][:, nb * 512:(nb + 1) * 512].rearrange("(k p) n -> p k n", p=128), writes=[w])
            P.dma('sp', b_[:], K.W['ada_b'][l:l + 1, nb * 512:(nb + 1) * 512], writes=[b_])
            for k in range(8):
                mm(P, p, p[:], G.condB[:, k, :], w[:, k, :], k == 0, False, [G.condB, w])
            mm(P, p, p[:], G.ones_f[0:1, :], b_[0:1, :], False, True, [G.ones_f, b_])
            cp(P, 'act', modB, modB[:, nb * 512:(nb + 1) * 512], p[:], [p])
        sl = lambda i: modB[:, i * D:(i + 1) * D]
        gn = lambda i: gains[:, i * D:(i + 1) * D]
        stt(P, G.A1, G.A1[:], sl(1), 1.0, gn(0), ALU.add, ALU.mult, [modB, gains])
        stt(P, G.A2, G.A2[:], sl(4), 1.0, gn(2), ALU.add, ALU.mult, [modB, gains])
        cp(P, 'act', G.B1, G.B1[:], sl(0), [modB])
        cp(P, 'act', G.B2, G.B2[:], sl(3), [modB])
        tt(P, 'dve', G.G1, G.G1[:], sl(2), gn(1), ALU.mult, [modB, gains])
        tt(P, 'dve', G.G2, G.G2[:], sl(5), gn(3), ALU.mult, [modB, gains])


def phase_normT(K, G, src, A, B, hT, hTd=None):
    P = K.P
    with Scope(K) as sc:
        xt = [sc.sb([128, D], F32) for _ in range(2)]; junk = sc.sb([128, D], F32)
        ss = [sc.sb([128, 1], F32) for _ in range(2)]; rs = [sc.sb([128, 1], F32) for _ in range(2)]
        tmp = [sc.sb([128, D], F32) for _ in range(2)]; hb = [sc.sb([128, D], BF16) for _ in range(2)]
        pt = [sc.ps([128, 8, 128], BF16) for _ in range(2)]
        for i in range(NT):
            x_ = xt[i % 2]; s_ = ss[i % 2]; r_ = rs[i % 2]; t_ = tmp[i % 2]; h_ = hb[i % 2]; p_ = pt[i % 2]
            P.dma('sp', x_[:], src[i * 128:(i + 1) * 128, :], writes=[x_])
            act(P, [junk, s_], junk[:], x_[:], AF.Square, [x_], accum_out=s_[:])
            act(P, r_, r_[:], s_[:], AF.Sqrt, [s_, G.eps6], scale=1.0 / D, bias=G.eps6[:])
            P.op('dve', lambda e, r_=r_: e.reciprocal(out=r_[:], in_=r_[:]), reads=[r_], writes=[r_])
            stt(P, t_, t_[:], x_[:], r_[:, 0:1], A[:], ALU.mult, ALU.mult, [x_, r_, A])
            tt(P, 'pool', h_, h_[:], t_[:], B[:], ALU.add, [t_, B])
            for k in range(8):
                tr(P, p_, p_[:, k, :], h_[:, k * 128:(k + 1) * 128], G.ident_b[:], [h_, G.ident_b])
            cp(P, 'act', hT, hT[:, :, i * 128:(i + 1) * 128], p_[:], [p_])
        if hTd is not None:
            for k in range(8):
                P.dma('sp', hTd[k * 128:(k + 1) * 128, :], hT[:, k, :], reads=[hT])


def load_w_cast(P, wt, dst_ap, src_ap):
    P.dma('pool', dst_ap, src_ap, writes=[wt])


def proj_fm(K, sc, hT, w2d, jobs, st_dt, ps_tiles):
    P = K.P
    wts = [sc.sb([128, 8, 128], BF16) for _ in range(2)]
    sts = [sc.sb([128, S], st_dt) for _ in range(2)]
    for ji, (pieces, ncw, dst) in enumerate(jobs):
        wt = wts[ji % 2]; st = sts[ji % 2]
        for (c0, n, do) in pieces:
            load_w_cast(P, wt, wt[:, :, do:do + n], w2d[:, c0:c0 + n].rearrange("(k p) n -> p k n", p=128))
        for tb in range(8):
            p = ps_tiles[tb % len(ps_tiles)]
            for k in range(8):
                mm(P, p, p[0:ncw, :], wt[:, k, 0:ncw], hT[:, k, tb * 512:(tb + 1) * 512], k == 0, k == 7, [wt, hT])
            cp(P, 'act' if tb % 2 == 0 else 'dve', st, st[0:ncw, tb * 512:(tb + 1) * 512], p[0:ncw, :], [p])
        P.dma('sp', dst, st[0:ncw, :], reads=[st])


def proj_tm(K, sc, hT, w2d, c0, ncols, ps_tiles, evac):
    P = K.P
    wt = sc.sb([128, 8, ncols], BF16)
    load_w_cast(P, wt, wt[:], w2d[:, c0:c0 + ncols].rearrange("(k p) n -> p k n", p=128))
    for i in range(NT):
        p = ps_tiles[i % len(ps_tiles)]
        for k in range(8):
            mm(P, p, p[:, 0:ncols], hT[:, k, i * 128:(i + 1) * 128], wt[:, k, :], k == 0, k == 7, [wt, hT])
        evac(i, p)


def phase_proj_mixer(K, G, l, hT, do_nsa=True):
    P = K.P
    w = K.W['w_in'][l]
    with Scope(K) as sc:
        ps_tiles = [sc.ps([128, 512], F32) for _ in range(4)]
        one = lambda c0, n, dst: ([(c0, n, 0)], n, dst)
        jobs = []
        if do_nsa:
            for b in range(4):
                jobs.append(one(b * 128, 128, K.qaT[b * 128:(b + 1) * 128, :]))
            jobs.append(one(512, 128, K.kvcT[0:128, :])); jobs.append(one(640, 128, K.kvcT[128:256, :]))
            jobs.append(one(768, 128, K.ksT[:, :])); jobs.append(one(1024, 128, K.kwT[:, :]))
            proj_fm(K, sc, hT, w, jobs, BF16, ps_tiles)
        jobs = [one(B0, 128, K.zbT[0:128, :]), one(B0 + 128, 128, K.zbT[128:256, :]), one(B0 + 256, 128, K.zbT[256:384, :]),
                ([(B0 + 384, 32, 0), (B0 + 384 + 16, 16, 32), (B0 + 384, 16, 48)], 64, K.zbT[384:448, :])]
        for b in range(14):
            jobs.append(one(C0 + b * 128, 128, K.zcT[b * 128:(b + 1) * 128, 1:S + 1]))
        proj_fm(K, sc, hT, w, jobs, F32, ps_tiles)
        z0 = sc.sb([128, 14], F32)
        P.op('dve', lambda e: e.memset(z0[:], 0.0), writes=[z0])
        P.dma('sp', K.zcT[:, 0:1].rearrange("(b p) o -> p (b o)", p=128), z0[:], reads=[z0], allow_slow_non_contiguous=True)
        if do_nsa:
            for (c0, dst) in ((896, K.vs), (1152, K.vw)):
                vst = sc.sb([128, NT, 2, 65], BF16)
                P.op('dve', lambda e, vst=vst: e.memset(vst[:], 1.0), writes=[vst])

                def ev(i, p, vst=vst):
                    cp(P, 'act' if i % 2 == 0 else 'dve', vst, vst[:, i, :, 0:64], p[:, 0:128].rearrange("p (g c) -> p g c", g=2), [p])
                proj_tm(K, sc, hT, w, c0, 128, ps_tiles, ev)
                P.dma('sp', dst.rearrange("(i p) g c -> p i g c", p=128), vst[:], reads=[vst])
            gst = sc.sb([128, NT, 24], F32)

            def evg(i, p):
                act(P, gst, gst[:, i, :], p[:, 0:24], AF.Sigmoid, [p])
            proj_tm(K, sc, hT, w, 1280, 24, ps_tiles, evg)
            P.dma('sp', K.gate.rearrange("(i p) c -> p i c", p=128), gst[:], reads=[gst])


def resid_epilogue(K, G, sc, T, i, py, Gg, xsrc, xdst):
    P = K.P
    x_ = T['x'][i % 2]; s2 = T['ss'][i % 2]; r_ = T['rs'][i % 2]; t_ = T['tmp'][i % 2]; junk = T['junk']
    P.dma('sp', x_[:], xsrc[i * 128:(i + 1) * 128, :], writes=[x_])
    for hf in range(2):
        act(P, [junk, s2], junk[:], py[:, hf, :], AF.Square, [py], accum_out=s2[:, hf:hf + 1])
    tt(P, 'dve', r_, r_[:], s2[:, 0:1], s2[:, 1:2], ALU.add, [s2])
    act(P, r_, r_[:], r_[:], AF.Sqrt, [r_, G.eps6], scale=1.0 / D, bias=G.eps6[:])
    P.op('dve', lambda e: e.reciprocal(out=r_[:], in_=r_[:]), reads=[r_], writes=[r_])
    for hf in range(2):
        stt(P, t_, t_[:, hf * 512:(hf + 1) * 512], py[:, hf, :], r_[:, 0:1], Gg[:, hf * 512:(hf + 1) * 512], ALU.mult, ALU.mult, [py, r_, Gg])
    tt(P, 'pool', x_, x_[:], x_[:], t_[:], ALU.add, [x_, t_])
    P.dma('sp', xdst[i * 128:(i + 1) * 128, :], x_[:], reads=[x_])


def epi_tiles(sc):
    return dict(x=[sc.sb([128, D], F32) for _ in range(2)], ss=[sc.sb([128, 2], F32) for _ in range(2)],
                rs=[sc.sb([128, 1], F32) for _ in range(2)], tmp=[sc.sb([128, D], F32) for _ in range(2)],
                junk=sc.sb([128, 512], F32))


def phase_merge(K, G, l, xsrc, xdst):
    P = K.P
    with Scope(K) as sc:
        wg = sc.sb([128, 8, 3 * D], BF16); wbr = sc.sb([128, 12, D], BF16); wo = sc.sb([128, 8, D], BF16)
        for c in range(6):
            load_w_cast(P, wg, wg[:, :, c * 512:(c + 1) * 512], K.W['w_in'][l][:, G0 + c * 512:G0 + (c + 1) * 512].rearrange("(k p) n -> p k n", p=128))
        for c in range(3):
            load_w_cast(P, wbr, wbr[:, c * 4:(c + 1) * 4, :], K.W['w_branch'][l][c * 512:(c + 1) * 512, :].rearrange("(k p) n -> p k n", p=128))
        for c in range(2):
            load_w_cast(P, wo, wo[:, c * 4:(c + 1) * 4, :], K.W['w_out'][l][c * 512:(c + 1) * 512, :].rearrange("(k p) n -> p k n", p=128))
        hb = [sc.sb([128, 8, 512], BF16) for _ in range(1)]; yb = [sc.sb([128, 12, 512], BF16) for _ in range(1)]
        mg = [sc.sb([128, 8, 512], BF16) for _ in range(1)]
        sg = [sc.sb([128, 512], F32) for _ in range(2)]; t1 = [sc.sb([128, 512], F32) for _ in range(2)]
        acc = [sc.sb([128, 512], F32) for _ in range(2)]
        pg = [sc.ps([128, 512], F32) for _ in range(2)]; pb = [sc.ps([128, 512], F32) for _ in range(2)]
        py = [sc.ps([128, 2, 512], F32) for _ in range(2)]
        T = epi_tiles(sc)
        cnt = 0
        for tb in range(8):
            P.barrier()
            h_ = hb[0]; y_ = yb[0]; m_ = mg[0]
            P.dma('sp', h_[:], K.hTd[:, tb * 512:(tb + 1) * 512].rearrange("(k p) t -> p k t", p=128), writes=[h_])
            for bi, src in enumerate((K.yaT, K.ybT, K.ycT)):
                P.dma('sp', y_[:, bi * 4:(bi + 1) * 4, :], src[:, tb * 512:(tb + 1) * 512].rearrange("(k p) t -> p k t", p=128), writes=[y_])
            for cb in range(8):
                a_ = acc[cb % 2]
                for br in range(3):
                    g_ = pg[cnt % 2]; b_ = pb[cnt % 2]; s_ = sg[cnt % 2]; t_ = t1[cnt % 2]; cnt += 1
                    for k in range(8):
                        mm(P, g_, g_[:], wg[:, k, br * D + cb * 128:br * D + (cb + 1) * 128], h_[:, k, :], k == 0, k == 7, [wg, h_])
                    act(P, s_, s_[:], g_[:], AF.Sigmoid, [g_])
                    for k in range(4):
                        mm(P, b_, b_[:], wbr[:, br * 4 + k, cb * 128:(cb + 1) * 128], y_[:, br * 4 + k, :], k == 0, k == 3, [wbr, y_])
                    if br == 0:
                        tt(P, 'dve', a_, a_[:], s_[:], b_[:], ALU.mult, [s_, b_])
                    else:
                        tt(P, 'dve', t_, t_[:], s_[:], b_[:], ALU.mult, [s_, b_])
                        if br == 1:
                            tt(P, 'pool', a_, a_[:], a_[:], t_[:], ALU.add, [a_, t_])
                        else:
                            tt(P, 'pool', m_, m_[:, cb, :], a_[:], t_[:], ALU.add, [a_, t_])
            for j in range(4):
                i = tb * 4 + j
                p_ = py[i % 2]
                for hf in range(2):
                    for k in range(8):
                        mm(P, p_, p_[:, hf, :], m_[:, k, j * 128:(j + 1) * 128], wo[:, k, hf * 512:(hf + 1) * 512], k == 0, k == 7, [m_, wo])
                resid_epilogue(K, G, sc, T, i, p_, G.G1, xsrc, xdst)


def phase_ffn(K, G, l, hT, xsrc, xdst):
    P = K.P
    with Scope(K) as sc:
        cw4 = [sc.sb([128, 1408], F32) for _ in range(2)]; cwT = sc.sb([128, 44, 4], F32)
        for c4 in cw4:
            P.op('dve', lambda e, c4=c4: e.memset(c4[:], 0.0), writes=[c4])
        pc = sc.ps([128, 44, 4], F32)
        for q in range(4):
            c4 = cw4[q % 2]
            P.dma('sp', c4[0:3, :], K.W['ffn_conv_w'][l][:, q * 1408:(q + 1) * 1408], writes=[c4])
            P.dma('sp', c4[3:4, :], K.W['ffn_conv_b'][l:l + 1, q * 1408:(q + 1) * 1408], writes=[c4])
            for b in range(11):
                tr(P, pc, pc[:, q * 11 + b, :], c4[:, b * 128:(b + 1) * 128], G.ident_f[:, 0:4], [c4, G.ident_f])
        cp(P, 'dve', cwT, cwT[:], pc[:], [pc])
        wts = [sc.sb([128, 8, 2, 128], BF16) for _ in range(2)]
        ug = [sc.sb([128, S + 2], F32)]; uv = [sc.sb([128, S + 2], F32)]
        c1 = sc.sb([128, S], F32); c2 = sc.sb([128, S], F32); gg = c1
        go = [sc.sb([128, S], BF16) for _ in range(2)]
        ps_ = [sc.ps([128, 512], F32) for _ in range(4)]
        for u in ug + uv:
            P.op('dve', lambda e, u=u: e.memset(u[:, 0:2], 0.0), writes=[u])
        up = K.W['ffn_up'][l]
        for fb in range(22):
            wt = wts[fb % 2]; g_ = ug[0]; v_ = uv[0]; o_ = go[fb % 2]
            load_w_cast(P, wt, wt[:, :, 0, :], up[:, fb * 128:(fb + 1) * 128].rearrange("(k p) n -> p k n", p=128))
            load_w_cast(P, wt, wt[:, :, 1, :], up[:, DFF + fb * 128:DFF + (fb + 1) * 128].rearrange("(k p) n -> p k n", p=128))
            for tb in range(8):
                for which, dstt, en in ((0, g_, 'act'), (1, v_, 'dve')):
                    p = ps_[(tb * 2 + which) % 4]
                    for k in range(8):
                        mm(P, p, p[:], wt[:, k, which, :], hT[:, k, tb * 512:(tb + 1) * 512], k == 0, k == 7, [wt, hT])
                    cp(P, en, dstt, dstt[:, 2 + tb * 512:2 + (tb + 1) * 512], p[:], [p])
            for (u_, cc, blk) in ((g_, c1, fb), (v_, c2, 22 + fb)):
                w_ = lambda kk, blk=blk: cwT[:, blk, kk:kk + 1]
                ts(P, 'dve', cc, cc[:], u_[:, 2:S + 2], w_(2), w_(3), ALU.mult, ALU.add, [u_, cwT])
                stt(P, cc, cc[:], u_[:, 1:S + 1], w_(1), cc[:], ALU.mult, ALU.add, [u_, cwT, cc])
                stt(P, cc, cc[:], u_[:, 0:S], w_(0), cc[:], ALU.mult, ALU.add, [u_, cwT, cc])
            act(P, gg, gg[:], c1[:], AF.Gelu, [c1])
            tt(P, 'pool', o_, o_[:], gg[:], c2[:], ALU.mult, [gg, c2])
            P.dma('sp', K.gT[fb * 128:(fb + 1) * 128, :], o_[:], reads=[o_])
    with Scope(K) as sc:
        wd = sc.sb([128, 22, D], BF16)
        for f in range(22):
            load_w_cast(P, wd, wd[:, f, :], K.W['ffn_down'][l][f * 128:(f + 1) * 128, :])
        gti = [sc.sb([128, 22, 128], BF16) for _ in range(2)]
        py = [sc.ps([128, 2, 512], F32) for _ in range(2)]
        T = epi_tiles(sc)
        for i in range(NT):
            g_ = gti[i % 2]; p_ = py[i % 2]
            P.dma('sp', g_[:], K.gT[:, i * 128:(i + 1) * 128].rearrange("(f p) t -> p f t", p=128), writes=[g_])
            for hf in range(2):
                for f in range(22):
                    mm(P, p_, p_[:, hf, :], g_[:, f, :], wd[:, f, hf * 512:(hf + 1) * 512], f == 0, f == 21, [g_, wd])
            resid_epilogue(K, G, sc, T, i, p_, G.G2, xsrc, xdst)


def attention(K, G, sc, R, qsrc, dk, kT, v, dv1, kbs_fn, active, extra, scale, finish, qbs=None):
    P = K.P
    for qb in (qbs if qbs is not None else range(8)):
        qt = R['qt'][qb % 2]
        if qsrc is not None:
            P.dma('sp', qt[0:dk, :], qsrc(qb), writes=[qt])
        kbs = kbs_fn(qb)
        firsts = {}; lasts = {}
        for kb in kbs:
            for j in range(4):
                if active(qb, kb, j):
                    firsts.setdefault(j, kb); lasts[j] = kb
        for kb in kbs:
            idx = R['cnt'][0]; R['cnt'][0] += 1
            psS = R['pss'][idx % 2]; pT = R['pT'][idx % 2]
            mms = [(kT[0:dk, kb * 128:(kb + 1) * 128], qt[0:dk, :], [kT, qt])] + extra(qb, kb)
            for mi, (lt, rh, rd) in enumerate(mms):
                mm(P, psS, psS[:], lt, rh, mi == 0, mi == len(mms) - 1, rd)
            act(P, pT, pT[:], psS[:], AF.Exp, [psS], scale=scale)
            for j in range(4):
                if active(qb, kb, j):
                    po = R['po'][j]
                    mm(P, po, po[:, 0:dv1], pT[:, j * 128:(j + 1) * 128], v[:, kb, 0:dv1], firsts[j] == kb, lasts[j] == kb, [pT, v])
        for j in range(4):
            finish(qb, j, R['po'][j])


def attn_tiles(sc, dkmax=96):
    return dict(qt=[sc.sb([dkmax, 512], BF16) for _ in range(2)], pss=[sc.ps([128, 512], F32) for _ in range(2)],
                pT=[sc.sb([128, 512], BF16) for _ in range(2)], po=[sc.ps([128, 512], F32) for _ in range(4)], cnt=[0])


def phase_mla(K, G, l):
    P = K.P
    with Scope(K) as sc:
        wuq = sc.sb([128, 2, 768], BF16); wsw = sc.sb([128, 2, 768], BF16); wukv = sc.sb([128, 1024], BF16)
        qn_g = sc.sb([128, 2], F32); kn_g = sc.sb([128, 1], F32)
        Wq = K.W['mla_w_uq'][l]
        load_w_cast(P, wuq, wuq[:], Wq.rearrange("(c p) n -> p c n", p=128))
        P.op('dve', lambda e: e.memset(wsw[:], 0.0), writes=[wsw])
        for c in range(2):
            dv = wsw[:, c, :].rearrange("p (h e) -> p h e", e=96)
            sv = Wq[c * 128:(c + 1) * 128, :].rearrange("k (h e) -> k h e", e=96)
            load_w_cast(P, wsw, dv[:, :, 64:80], sv[:, :, 80:96])
            load_w_cast(P, wsw, dv[:, :, 80:96], sv[:, :, 64:80])
        load_w_cast(P, wukv, wukv[:], K.W['mla_w_ukv'][l])
        P.dma('sp', qn_g[:], K.W['mla_q_norm'][l:l + 1, :].rearrange("o (c p) -> p (o c)", p=128), writes=[qn_g], allow_slow_non_contiguous=True)
        P.dma('sp', kn_g[:], K.W['mla_kv_norm'][l:l + 1, :].rearrange("o (c p) -> p (o c)", p=128), writes=[kn_g], allow_slow_non_contiguous=True)
        eps = G.eps6
        qa = [sc.sb([128, 2, 512], F32) for _ in range(2)]; kva = [sc.sb([128, 512], F32) for _ in range(2)]
        krr = [sc.sb([96, 2, 512], F32) for _ in range(2)]; tab = [sc.sb([96, 2, 512], F32) for _ in range(2)]
        sq = sc.sb([128, 2, 512], F32); rq = sc.sb([128, 512], F32); rk = sc.sb([128, 512], F32)
        qn = sc.sb([128, 2, 512], BF16); kvn = sc.sb([128, 512], BF16)
        t1 = sc.sb([96, 512], F32); t2 = sc.sb([96, 512], F32)
        qh = [sc.sb([96, 512], BF16) for _ in range(2)]; kh = [sc.sb([64, 512], BF16) for _ in range(2)]
        kro = sc.sb([96, 512], BF16)
        vst = [sc.sb([128, 4, 8, 65], BF16) for _ in range(2)]
        for v_ in vst:
            P.op('dve', lambda e, v_=v_: e.memset(v_[:], 1.0), writes=[v_])
        pss = sc.ps([128, 512], F32); p1 = [sc.ps([128, 512], F32) for _ in range(2)]; p2 = [sc.ps([128, 512], F32) for _ in range(2)]
        p3 = [sc.ps([128, 512], F32) for _ in range(2)]
        for tb in range(8):
            tsl = slice(tb * 512, (tb + 1) * 512)
            qa_ = qa[tb % 2]; kva_ = kva[tb % 2]; krr_ = krr[tb % 2]; tab_ = tab[tb % 2]; vst_ = vst[tb % 2]
            P.dma('sp', qa_[:], K.zbT[0:256, tsl].rearrange("(c p) t -> p c t", p=128), writes=[qa_])
            P.dma('sp', kva_[:], K.zbT[256:384, tsl], writes=[kva_])
            P.dma('sp', krr_[64:96, :, :], K.zbT[384:448, tsl].rearrange("(two r) t -> r two t", two=2), writes=[krr_])
            P.dma('sp', tab_[64:96, :, :], K.C['rope'][:, :, tsl].rearrange("two r t -> r two t"), writes=[tab_])
            act(P, sq, sq[:], qa_[:], AF.Square, [qa_])
            for c in range(2):
                mm(P, pss, pss[:], G.ones_f[:], sq[:, c, :], c == 0, c == 1, [G.ones_f, sq])
            act(P, rq, rq[:], pss[:], AF.Sqrt, [pss, eps], scale=1.0 / 256, bias=eps[:])
            P.op('dve', lambda e: e.reciprocal(out=rq[:], in_=rq[:]), reads=[rq], writes=[rq])
            for c in range(2):
                stt(P, qn, qn[:, c, :], qa_[:, c, :], qn_g[:, c:c + 1], rq[:], ALU.mult, ALU.mult, [qa_, qn_g, rq])
            act(P, sq, sq[:, 0, :], kva_[:], AF.Square, [kva_])
            mm(P, pss, pss[:], G.ones_f[:], sq[:, 0, :], True, True, [G.ones_f, sq])
            act(P, rk, rk[:], pss[:], AF.Sqrt, [pss, eps], scale=1.0 / 128, bias=eps[:])
            P.op('dve', lambda e: e.reciprocal(out=rk[:], in_=rk[:]), reads=[rk], writes=[rk])
            stt(P, kvn, kvn[:], kva_[:], kn_g[:, 0:1], rk[:], ALU.mult, ALU.mult, [kva_, kn_g, rk])
            tt(P, 'dve', t1, t1[64:96, :], krr_[64:96, 0, :], tab_[64:96, 0, :], ALU.mult, [krr_, tab_])
            tt(P, 'dve', t2, t2[64:96, :], krr_[64:96, 1, :], tab_[64:96, 1, :], ALU.mult, [krr_, tab_])
            tt(P, 'pool', kro, kro[64:96, :], t1[64:96, :], t2[64:96, :], ALU.add, [t1, t2])
            for h in range(8):
                P.dma('sp', K.mkT[h, 64:96, tsl], kro[64:96, :], reads=[kro])
            for h in range(8):
                a_ = p1[h % 2]; b_ = p2[h % 2]; c_ = p3[h % 2]; qh_ = qh[h % 2]; kh_ = kh[h % 2]
                for c in range(2):
                    mm(P, a_, a_[0:96, :], wuq[:, c, h * 96:(h + 1) * 96], qn[:, c, :], c == 0, c == 1, [wuq, qn])
                for c in range(2):
                    mm(P, b_, b_[0:96, :], wsw[:, c, h * 96:(h + 1) * 96], qn[:, c, :], c == 0, c == 1, [wsw, qn])
                cp(P, 'act', qh_, qh_[0:64, :], a_[0:64, :], [a_])
                tt(P, 'dve', t1, t1[64:96, :], a_[64:96, :], tab_[64:96, 0, :], ALU.mult, [a_, tab_])
                tt(P, 'dve', t2, t2[64:96, :], b_[64:96, :], tab_[64:96, 1, :], ALU.mult, [b_, tab_])
                tt(P, 'pool', qh_, qh_[64:96, :], t1[64:96, :], t2[64:96, :], ALU.add, [t1, t2])
                P.dma('sp', K.mqT[h, :, tsl], qh_[:], reads=[qh_])
                mm(P, c_, c_[0:64, :], wukv[:, h * 128:h * 128 + 64], kvn[:], True, True, [wukv, kvn])
                cp(P, 'act', kh_, kh_[:], c_[0:64, :], [c_])
                P.dma('sp', K.mkT[h, 0:64, tsl], kh_[:], reads=[kh_])
            for j in range(4):
                c_ = p3[j % 2]
                mm(P, c_, c_[:].rearrange("p (h e) -> p h e", e=64), kvn[:, j * 128:(j + 1) * 128],
                   wukv[:].rearrange("p (h e) -> p h e", e=128)[:, :, 64:128], True, True, [wukv, kvn])
                cp(P, 'act' if j % 2 else 'dve', vst_, vst_[:, j, :, 0:64], c_[:].rearrange("p (h e) -> p h e", e=64), [c_])
            P.dma('sp', K.mv[tsl].rearrange("(j p) h c -> p j h c", p=128), vst_[:], reads=[vst_])
    with Scope(K) as sc:
        R = attn_tiles(sc, 96)
        kT = [sc.sb([96, S], BF16) for _ in range(2)]; v = [sc.sb([128, NT, 65], BF16) for _ in range(2)]
        cmk = sc.sb([128, 4, 512], BF16)
        load_w_cast(P, cmk, cmk[:], K.C['cmask'].rearrange("o k q -> k o q"))
        rs = [sc.sb([128, 1], F32) for _ in range(2)]; on = [sc.sb([128, 64], BF16) for _ in range(2)]
        ptT = [sc.ps([64, 128], BF16) for _ in range(2)]; yst = [sc.sb([64, 512], BF16) for _ in range(2)]
        cnt = [0]
        for h in range(8):
            P.barrier()
            kT_ = kT[h % 2]; v_ = v[h % 2]
            P.dma('sp', kT_[:], K.mkT[h], writes=[kT_])
            P.dma('sp', v_[:], K.mv[:, h, :].rearrange("(i p) c -> p i c", p=128), writes=[v_])

            def extra(qb, kb):
                o = kb - qb * 4
                if 0 <= o < 4:
                    return [(G.ident_b[:], cmk[:, o, :], [G.ident_b, cmk])]
                return []

            def finish(qb, j, po, h=h):
                i = cnt[0]; cnt[0] += 1
                r_ = rs[i % 2]; o_ = on[i % 2]; t_ = ptT[i % 2]; y_ = yst[qb % 2]
                P.op('dve', lambda e: e.reciprocal(out=r_[:], in_=po[:, 64:65]), reads=[po], writes=[r_])
                ts(P, 'dve', o_, o_[:], po[:, 0:64], r_[:, 0:1], None, ALU.mult, None, [po, r_])
                tr(P, t_, t_[:], o_[:], G.ident_b[:], [o_, G.ident_b])
                cp(P, 'act', y_, y_[:, j * 128:(j + 1) * 128], t_[:], [t_])
                if j == 3:
                    P.dma('sp', K.ybT[h * 64:(h + 1) * 64, qb * 512:(qb + 1) * 512], y_[:], reads=[y_])
            attention(K, G, sc, R, lambda qb, h=h: K.mqT[h, :, qb * 512:(qb + 1) * 512], 96, kT_, v_, 65,
                      lambda qb: list(range(0, qb * 4 + 4)), lambda qb, kb, j: kb <= qb * 4 + j, extra,
                      96 ** -0.5, finish)


class GScope(Scope):
    def __exit__(self, *a):
        return self.st.__exit__(*a)


def run_all(K, gst, stages, nlayers):
    P = K.P
    G = GScope(K); gst.enter_context(G)
    stages = stages or {'ada', 'normT', 'proj', 'mla', 'rwkv', 'nsa', 'merge', 'ffn'}
    phase_init(K, G)
    P.barrier()
    if 'nsa' in stages:
        phase_tables(K, G)
    for l in range(nlayers):
        xin = K.x_in if l == 0 else K.xa
        xout = K.out if l == nlayers - 1 else K.xa
        if 'ada' in stages:
            phase_ada(K, G, l)
        if 'normT' in stages:
            with Scope(K) as sc:
                hT = sc.sb([128, 8, S], BF16)
                phase_normT(K, G, xin, G.A1, G.B1, hT, K.hTd)
                if 'proj' in stages:
                    phase_proj_mixer(K, G, l, hT, do_nsa=('nsa' in stages))
        if 'mla' in stages:
            phase_mla(K, G, l)
        if 'rwkv' in stages:
            phase_rwkv(K, G, l)
        if 'nsa' in stages:
            phase_nsa(K, G, l)
        if 'merge' in stages:
            phase_merge(K, G, l, xin, K.xb)
        if 'ffn' in stages:
            with Scope(K) as sc:
                hT = sc.sb([128, 8, S], BF16)
                phase_normT(K, G, K.xb, G.A2, G.B2, hT)
                phase_ffn(K, G, l, hT, K.xb, xout)


GN_EPS = 64e-5


def phase_rwkv(K, G, l):
    P = K.P
    W = K.W
    import os
    if 'RW_MAXOPS' in os.environ:
        P.limit = int(os.environ['RW_MAXOPS']); P.nops = 0
    mark = lambda n: print('MARK', n, getattr(P, 'nops', None)) if 'RW_MAXOPS' in os.environ else None
    with Scope(K) as sc:
        H = 8
        msk = sc.sb([64, 4, 512], F32); rreset = sc.sb([64, 512], F32)
        P.dma('sp', msk[:], K.C['rmask'].rearrange("m p f -> p m f"), writes=[msk])
        P.dma('sp', rreset[:], K.C['rreset'], writes=[rreset])
        m3 = lambda i: msk[:, i, :].rearrange("p (h t) -> p h t", h=H)
        mu3 = sc.sb([64, 3, H], F32); muw = sc.sb([64, 1], F32); mua = sc.sb([64, 1], F32); mug = sc.sb([128, 1], F32)
        nc_ = dict(allow_slow_non_contiguous=True)
        P.dma('sp', mu3[:], W['rwkv_mu'][l:l + 1, 0:1536].rearrange("o (q h j) -> j (o q) h", q=3, h=H), writes=[mu3], **nc_)
        P.dma('sp', muw[:], W['rwkv_mu'][l:l + 1, 1536:1600].rearrange("o j -> j o"), writes=[muw], **nc_)
        P.dma('sp', mua[:], W['rwkv_mu'][l:l + 1, 1600:1664].rearrange("o j -> j o"), writes=[mua], **nc_)
        P.dma('sp', mug[:], W['rwkv_mu'][l:l + 1, 1664:1792].rearrange("o j -> j o"), writes=[mug], **nc_)
        pv = {}
        for nm in ('rwkv_w0', 'rwkv_a0', 'rwkv_k_k', 'rwkv_k_a'):
            t = sc.sb([64, H], F32)
            P.dma('sp', t[:], W[nm][l:l + 1, :].rearrange("o (h j) -> j (o h)", h=H), writes=[t], **nc_)
            pv[nm] = t
        rk_ = sc.sb([64, H], F32)
        P.dma('sp', rk_[:], W['rwkv_r_k'][l].rearrange("h j -> j h"), writes=[rk_], **nc_)
        lnw = sc.sb([64, 512], F32); lnb = sc.sb([64, 512], F32)
        P.dma('sp', lnw[:], W['rwkv_ln'][l, 0:1, :].broadcast_to([64, 512]), writes=[lnw])
        P.dma('sp', lnb[:], W['rwkv_ln'][l, 1:2, :].broadcast_to([64, 512]), writes=[lnb])
        w2 = sc.sb([64, 576], F32); a2 = sc.sb([64, 576], F32); g2 = sc.sb([128, 512], F32)
        P.op('dve', lambda e: e.memset(w2[:, 512:576], 0.0), writes=[w2]); P.op('dve', lambda e: e.memset(a2[:, 512:576], 0.0), writes=[a2])
        P.dma('sp', w2[:, 0:512], W['rwkv_w2'][l], writes=[w2]); P.dma('sp', a2[:, 0:512], W['rwkv_a2'][l], writes=[a2])
        P.dma('sp', g2[:], W['rwkv_g2'][l], writes=[g2])
        bc = lambda t: t[:, :].unsqueeze(2).broadcast_to([64, H, 64])
        idf = G.ident_f[0:64, 0:64]

        gp = [sc.ps([128, 512], F32) for _ in range(3)]
        pLs = [sc.ps([128, 512], F32) for _ in range(2)]; pY = sc.ps([128, 512], F32); pS = sc.ps([128, 512], F32)
        pLo = lambda h: pLs[h // 4][0:64, (h % 4) * 128:(h % 4 + 1) * 128]
        pLv = lambda hf: pLs[hf][0:64, :].rearrange("p (h t) -> p h t", h=4)
        hs = lambda hf: slice(4 * hf, 4 * hf + 4)
        gi = [0]

        def nextp():
            gi[0] += 1
            return gp[gi[0] % 3]
        v3 = lambda p: p[0:64, :].rearrange("p (h t) -> p h t", h=H)
        o64 = lambda p, h: p[0:64, h * 64:(h + 1) * 64]

        class T3c(Tn):
            def __init__(self, tn):
                self.t = tn.t; self.b = tn.b
                self.v = self.t[:, 0:512].rearrange("p (h t) -> p h t", h=H)

            def __getitem__(self, k):
                return self.v[k]

            def L(self, h):
                return self.t[:, h * 64:(h + 1) * 64]

        def T3(n=1, dt=F32, pad=True):
            r = []
            for _ in range(n):
                x = T3c(sc.sb([64, 576 if pad else 512], dt))
                if pad:
                    P.op('dve', lambda e, x=x: e.memset(x.t[:, 512:576], 0.0), writes=[x])
                r.append(x)
            return r if n > 1 else r[0]
        S0 = T3(2)
        P.op('dve', lambda e: e.memset(S0[0][:], 0.0), writes=[S0[0]])
        GS = 128
        zin = [[sc.sb([64, H, GS + 1], F32) for _ in range(3)]]
        zw = [sc.sb([64, GS + 1], F32) for _ in range(1)]; za = [sc.sb([64, GS + 1], F32) for _ in range(1)]
        zg = [sc.sb([128, GS + 1], F32) for _ in range(1)]
        dtmp = sc.sb([64, H, GS], F32)
        wd = sc.sb([64, GS], F32); ad = sc.sb([64, GS], F32); gd = sc.sb([128, GS + 64], F32); d1 = sc.sb([128, GS], F32)
        P.op('dve', lambda e: e.memset(gd[:, GS:GS + 64], 0.0), writes=[gd])
        ycst = [sc.sb([64, H, GS], BF16) for _ in range(2)]
        wl, logd, a_, kk, sqk, rn, kmod, b_, cum, Dincl, Dexcl, Dinv, tmp2 = [T3(pad=False) for _ in range(13)]
        tm = T3()
        AR = sc.sb([64, H, 2, 64], F32); bT = T3(); kT = T3(); bDT = T3(); kDT = T3()
        AW = sc.sb([64, H, 2, 64], F32); AU = sc.sb([64, H, 2, 64], F32)
        BD, Vt = T3(pad=False), T3(pad=False)
        NBD, KD, Vf = T3(), T3(), T3()
        Nm, NTm, MbT, NMbT, LkT, MkT = [T3() for _ in range(6)]
        Pk = T3(2); PkT = T3(2); Z = T3(2)
        QT = T3(); GT = T3(); dgD = T3(pad=False)
        mn = sc.sb([64, H], F32); var = sc.sb([64, H], F32); bon = sc.sb([64, H], F32)
        yc = T3(pad=False); ysq = T3(pad=False); ysb = T3(pad=False); yo = sc.sb([64, 512], F32)
        sq128 = sc.sb([128, 512], F32); eps24 = sc.sb([64, 1], F32); DC = sc.sb([64, H], F32)
        P.op('dve', lambda e: e.memset(sq128[:], 0.0), writes=[sq128]); P.op('dve', lambda e: e.memset(eps24[:], 1e-24), writes=[eps24])
        f3 = lambda t: t[:, :].rearrange("p (h t) -> p h t", h=H)
        CUT = None
        NG = S // GS if CUT is None else 1
        mark('groups start')
        for g in range(NG):
            if g % 4 == 0 and g > 0:
                P.barrier()
            t0 = g * GS
            zi = zin[0]
            for q in range(3):
                P.dma('sp', zi[q][:], K.zcT[q * 512:(q + 1) * 512, t0:t0 + GS + 1].rearrange("(h j) t -> j h t", j=64), writes=[zi[q]])
            zw_ = zw[0]; za_ = za[0]; zg_ = zg[0]
            P.dma('sp', zw_[:], K.zcT[1536:1600, t0:t0 + GS + 1], writes=[zw_])
            P.dma('sp', za_[:], K.zcT[1600:1664, t0:t0 + GS + 1], writes=[za_])
            P.dma('sp', zg_[:], K.zcT[1664:1792, t0:t0 + GS + 1], writes=[zg_])
            for q in range(3):
                tt(P, 'dve', dtmp, dtmp[:], zi[q][:, :, 0:GS], zi[q][:, :, 1:GS + 1], ALU.subtract, [zi[q]])
                tt(P, 'dve', dtmp, dtmp[:], dtmp[:], mu3[:, q, :].unsqueeze(2).broadcast_to([64, H, GS]), ALU.mult, [dtmp, mu3])
                tt(P, 'pool', zi[q], zi[q][:, :, 1:GS + 1], dtmp[:], zi[q][:, :, 1:GS + 1], ALU.add, [dtmp, zi[q]])
            for (zz, mu_, dst, np_) in ((zw_, muw, wd, 64), (za_, mua, ad, 64), (zg_, mug, gd, 128)):
                tt(P, 'dve', d1, d1[0:np_, :], zz[:, 0:GS], zz[:, 1:GS + 1], ALU.subtract, [zz])
                stt(P, dst, dst[:, 0:GS], d1[0:np_, :], mu_[:, 0:1], zz[:, 1:GS + 1], ALU.mult, ALU.add, [d1, mu_, zz])
            act(P, wd, wd[:, 0:GS], wd[:, 0:GS], AF.Tanh, [wd])
            act(P, gd, gd[:, 0:GS], gd[:, 0:GS], AF.Sigmoid, [gd])
            yst = ycst[g % 2]
            for c in range(GS // 64):
                cs = slice(c * 64, (c + 1) * 64); cs1 = slice(1 + c * 64, 1 + (c + 1) * 64)
                r = zi[0][:, :, cs1]; k = zi[1][:, :, cs1]; v = zi[2][:, :, cs1]
                rd = [zi[0], zi[1], zi[2]]
                mark('decay / a')
                p = nextp()
                for h in range(H):
                    mm(P, p, o64(p, h), w2[:, h * 64:(h + 1) * 64], wd[:, cs], True, True, [w2, wd])
                tt(P, 'dve', wl, wl[:], v3(p), bc(pv['rwkv_w0']), ALU.add, [p, pv['rwkv_w0']])
                act(P, logd, logd[:], wl[:], AF.Sigmoid, [wl])
                ts(P, 'pool', logd, logd[:], logd[:], -0.6065306597126334, None, ALU.mult, None, [logd])
                p = nextp()
                for h in range(H):
                    mm(P, p, o64(p, h), a2[:, h * 64:(h + 1) * 64], ad[:, cs], True, True, [a2, ad])
                tt(P, 'dve', wl, wl[:], v3(p), bc(pv['rwkv_a0']), ALU.add, [p, pv['rwkv_a0']])
                act(P, a_, a_[:], wl[:], AF.Sigmoid, [wl])
                if CUT is not None and CUT < 1:
                    continue
                mark('kk normalised')
                tt(P, 'dve', kk, kk[:], k, bc(pv['rwkv_k_k']), ALU.mult, rd + [pv['rwkv_k_k']])
                act(P, sq128, sq128[0:64, :], kk.t[:, 0:512], AF.Square, [kk])
                p = nextp()
                mm(P, p, p[:, :], G.ones_f[:, :], sq128[:, :], True, True, [G.ones_f, sq128])
                act(P, rn, rn.t[:, 0:512], p[0:64, :], AF.Sqrt, [p, eps24], scale=1.0, bias=eps24[:])
                P.op('dve', lambda e: e.reciprocal(out=rn.t[:, 0:512], in_=rn.t[:, 0:512]), reads=[rn], writes=[rn])
                tt(P, 'dve', kk, kk[:], kk[:], rn[:], ALU.mult, [kk, rn])
                if CUT is not None and CUT < 2:
                    continue
                mark('k modified, b')
                ts(P, 'dve', tm, tm[:], a_[:], -1.0, None, ALU.add, None, [a_])
                tt(P, 'dve', tm, tm[:], tm[:], bc(pv['rwkv_k_a']), ALU.mult, [tm, pv['rwkv_k_a']])
                ts(P, 'dve', tm, tm[:], tm[:], 1.0, None, ALU.add, None, [tm])
                tt(P, 'dve', kmod, kmod[:], tm[:], k, ALU.mult, [tm] + rd)
                tt(P, 'pool', b_, b_[:], kk[:], a_[:], ALU.mult, [kk, a_])
                if CUT is not None and CUT < 3:
                    continue
                mark('cumulative decay')
                P.op('dve', lambda e: e.tensor_tensor_scan(out=cum.t[:, 0:512], data0=rreset[:],
                                                           data1=logd.t[:, 0:512], initial=0.0,
                                                           op0=ALU.mult, op1=ALU.add), reads=[rreset, logd], writes=[cum])
                act(P, Dincl, Dincl[:], cum[:], AF.Exp, [cum])
                tt(P, 'dve', tmp2, tmp2[:], cum[:], logd[:], ALU.subtract, [cum, logd])
                act(P, Dexcl, Dexcl[:], tmp2[:], AF.Exp, [tmp2])
                P.op('dve', lambda e: e.reciprocal(out=Dinv.t[:, 0:512], in_=Dincl.t[:, 0:512]), reads=[Dincl], writes=[Dinv])
                cp(P, 'act', DC, DC[:], Dincl[:, :, 63], [Dincl])
                DCb = bc(DC)
                tt(P, 'dve', AR, AR[:, :, 0, :], kk[:], Dexcl[:], ALU.mult, [kk, Dexcl])
                tt(P, 'dve', AR, AR[:, :, 1, :], r, Dincl[:], ALU.mult, rd + [Dincl])
                tt(P, 'pool', bT, bT[:], b_[:], Dinv[:], ALU.mult, [b_, Dinv])
                tt(P, 'pool', kT, kT[:], kmod[:], Dinv[:], ALU.mult, [kmod, Dinv])
                tt(P, 'dve', bDT, bDT[:], bT[:], DCb, ALU.mult, [bT, DC])
                tt(P, 'dve', kDT, kDT[:], kT[:], DCb, ALU.mult, [kT, DC])
                tt(P, 'dve', dgD, dgD[:], m3(3), DCb, ALU.mult, [msk, DC])
                if CUT is not None and CUT < 4:
                    continue
                mark('token-major transposes')
                cp(P, 'act', Vf, Vf[:], v, rd)
                for (src_ap, srct, dstt, dst_ap, neg) in ((lambda h: AR[:, h, 0, :], AR, AW, AW[:, :, 0, :], None),
                                                         (lambda h: bDT.L(h), bDT, BD, BD[:], NBD),
                                                         (lambda h: kDT.L(h), kDT, KD, KD[:], None),
                                                         (lambda h: Vf.L(h), Vf, Vt, Vt[:], None)):
                    p = nextp()
                    for h in range(H):
                        tr(P, p, o64(p, h), src_ap(h), idf, [srct, G.ident_f])
                    cp(P, 'act', dstt, dst_ap, v3(p), [p])
                    if neg is not None:
                        ts(P, 'dve', neg, neg[:], dstt[:], -1.0, None, ALU.mult, None, [dstt])
                if CUT is not None and CUT < 5:
                    continue
                mark('L / M blocks')
                for h in range(H):
                    mm(P, pLs[h // 4], pLo(h), bT.L(h), AR[:, h, :, :].rearrange("p a t -> p (a t)"), True, True, [bT, AR])
                for hf in range(2):
                    tt(P, 'dve', Nm, Nm[:, hs(hf), :], pLv(hf)[:, :, 0:64], m3(0)[:, hs(hf), :], ALU.mult, [pLs[hf], msk])
                    tt(P, 'dve', MbT, MbT[:, hs(hf), :], pLv(hf)[:, :, 64:128], m3(1)[:, hs(hf), :], ALU.mult, [pLs[hf], msk])
                ts(P, 'pool', NMbT, NMbT[:], MbT[:], -1.0, None, ALU.mult, None, [MbT])
                for h in range(H):
                    mm(P, pLs[h // 4], pLo(h), kT.L(h), AR[:, h, :, :].rearrange("p a t -> p (a t)"), True, True, [kT, AR])
                for hf in range(2):
                    tt(P, 'dve', LkT, LkT[:, hs(hf), :], pLv(hf)[:, :, 0:64], m3(0)[:, hs(hf), :], ALU.mult, [pLs[hf], msk])
                    tt(P, 'dve', MkT, MkT[:, hs(hf), :], pLv(hf)[:, :, 64:128], m3(1)[:, hs(hf), :], ALU.mult, [pLs[hf], msk])
                p = nextp()
                for h in range(H):
                    mm(P, p, o64(p, h), AR[:, h, 0, :], bT[:, h, :], True, True, [AR, bT])
                tt(P, 'dve', NTm, NTm[:], v3(p), m3(2), ALU.mult, [p, msk])
                if CUT is not None and CUT < 6:
                    continue
                mark('(I+N)^-1 by doubling')
                tt(P, 'dve', Z[0], Z[0][:], m3(3), Nm[:], ALU.subtract, [msk, Nm])
                Pc, PcT, Zc = Nm, NTm, Z[0]
                for lev in range(1, 6):
                    Pn = Pk[lev % 2]; PnT = PkT[lev % 2]; Zn = Z[lev % 2]
                    if lev < 5:
                        p = nextp()
                        for h in range(H):
                            mm(P, p, o64(p, h), PcT.L(h), Pc[:, h, :], True, True, [PcT, Pc])
                        cp(P, 'act', Pn, Pn[:], v3(p), [p])
                    p = nextp()
                    for h in range(H):
                        mm(P, p, o64(p, h), Pc.L(h), PcT[:, h, :], True, True, [PcT, Pc])
                    cp(P, 'act', PnT, PnT[:], v3(p), [p])
                    p = nextp()
                    for h in range(H):
                        mm(P, p, o64(p, h), PnT.L(h), Zc[:, h, :], True, True, [PnT, Zc])
                    tt(P, 'dve', Zn, Zn[:], Zc[:], v3(p), ALU.add, [Zc, p])
                    Pc, PcT, Zc = Pn, PnT, Zn
                TT = Zc
                if CUT is not None and CUT < 7:
                    continue
                mark('W_v, [A~ | U_v]')
                p = nextp()
                for h in range(H):
                    mm(P, p, o64(p, h), LkT.L(h), Vt[:, h, :], True, True, [LkT, Vt])
                cp(P, 'act', AW, AW[:, :, 1, :], v3(p), [p])
                for h in range(H):
                    mm(P, pLs[h // 4], pLo(h), TT.L(h), AW[:, h, :, :].rearrange("p a t -> p (a t)"), True, True, [TT, AW])
                for hf in range(2):
                    cp(P, 'act', AU, AU[:, hs(hf), :, :].rearrange("p h a t -> p h (a t)"), pLv(hf), [pLs[hf]])
                p = nextp()
                for h in range(H):
                    mm(P, p, o64(p, h), AU[:, h, 0, :], MbT[:, h, :], True, True, [AU, MbT])
                tt(P, 'dve', QT, QT[:], AR[:, :, 1, :], v3(p), ALU.subtract, [AR, p])
                p = nextp()
                for h in range(H):
                    mm(P, p, o64(p, h), AU[:, h, 0, :], BD[:, h, :], True, True, [AU, BD])
                tt(P, 'dve', GT, GT[:], dgD[:], v3(p), ALU.subtract, [dgD, p])
                Sc = S0[(g * (GS // 64) + c) % 2]; Sn = S0[(g * (GS // 64) + c + 1) % 2]
                pY3 = v3(pY); pS3 = v3(pS)
                for h in range(H):
                    mm(P, pY, o64(pY, h), MkT.L(h), Vt[:, h, :], True, False, [MkT, Vt])
                    mm(P, pY, o64(pY, h), NMbT.L(h), AU[:, h, 1, :], False, False, [NMbT, AU])
                    mm(P, pY, o64(pY, h), QT.L(h), Sc[:, h, :], False, True, [QT, Sc])
                for h in range(H):
                    mm(P, pS, o64(pS, h), KD.L(h), Vt[:, h, :], True, False, [KD, Vt])
                    mm(P, pS, o64(pS, h), NBD.L(h), AU[:, h, 1, :], False, False, [NBD, AU])
                    mm(P, pS, o64(pS, h), GT.L(h), Sc[:, h, :], False, True, [GT, Sc])
                cp(P, 'act', Sn, Sn[:], pS3, [pS])
                if CUT is not None and CUT < 8:
                    continue
                mark('group norm + bonus + gate')
                cp(P, 'act', ysb, ysb[:], pY3, [pY])
                P.op('dve', lambda e: e.tensor_reduce(out=mn[:], in_=ysb[:], axis=AX.X, op=ALU.add), reads=[ysb], writes=[mn])
                ts(P, 'dve', mn, mn[:], mn[:], -1.0 / 64, None, ALU.mult, None, [mn])
                tt(P, 'dve', yc, yc[:], ysb[:], bc(mn), ALU.add, [ysb, mn])
                act(P, ysq, ysq[:], yc[:], AF.Square, [yc])
                P.op('dve', lambda e: e.tensor_reduce(out=var[:], in_=ysq[:], axis=AX.X, op=ALU.add), reads=[ysq], writes=[var])
                ts(P, 'dve', var, var[:], var[:], 1.0 / 64, GN_EPS, ALU.mult, ALU.add, [var])
                act(P, var, var[:], var[:], AF.Sqrt, [var])
                P.op('dve', lambda e: e.reciprocal(out=var[:], in_=var[:]), reads=[var], writes=[var])
                tt(P, 'dve', yc, yc[:], yc[:], bc(var), ALU.mult, [yc, var])
                tt(P, 'dve', yc, yc[:], yc[:], f3(lnw), ALU.mult, [yc, lnw])
                tt(P, 'pool', yc, yc[:], yc[:], f3(lnb), ALU.add, [yc, lnb])
                tt(P, 'pool', tm, tm[:], r, kmod[:], ALU.mult, rd + [kmod])
                tt(P, 'dve', tm, tm[:], tm[:], bc(rk_), ALU.mult, [tm, rk_])
                p = nextp()
                for h in range(H):
                    mm(P, p, o64(p, h), tm.L(h), G.ones_f[0:64, 0:64], True, True, [tm, G.ones_f])
                cp(P, 'act', bon, bon[:], v3(p)[:, :, 0], [p])
                tt(P, 'dve', ysq, ysq[:], Vt[:], bc(bon), ALU.mult, [Vt, bon])
                tt(P, 'dve', yc, yc[:], yc[:], ysq[:], ALU.add, [yc, ysq])
                p = nextp()
                mm(P, p, p[:, :], gd[:, c * 64:c * 64 + 128], g2[:], True, True, [gd, g2])
                tt(P, 'dve', yo, yo[:], yc.t[:, 0:512], p[0:64, :], ALU.mult, [yc, p])
                p = nextp()
                for h in range(H):
                    tr(P, p, o64(p, h), yo[:, h * 64:(h + 1) * 64], idf, [yo, G.ident_f])
                cp(P, 'act', yst, yst[:, :, cs], v3(p), [p])
            P.dma('sp', K.ycT[:, t0:t0 + GS].rearrange("(h j) t -> j h t", j=64), yst[:], reads=[yst])


def phase_tables(K, G):
    P = K.P
    with Scope(K) as sc:
        rb = sc.sb([32, 128], F32); oh = sc.sb([32, 128], F32)
        P.op('dve', lambda e: e.memset(rb[:], 0.0), writes=[rb])
        P.dma('sp', rb[:, 0:8], K.rel_bias, writes=[rb]); P.dma('sp', oh[:], K.C['t5oh'], writes=[oh])
        ptf_ = sc.ps([128, 128], F32)
        mm(P, ptf_, ptf_[:], rb[:], oh[:], True, True, [rb, oh])
        pt = ptf_
        pt_v = ptf_[0:8, :]
        tb = sc.sb([8, TABL], F32); tbb = sc.sb([8, TABL], BF16)
        P.op('dve', lambda e: e.memset(tb[:], 0.0), writes=[tb])
        P.op('dve', lambda e: e.memset(tb[:, 0:OFF], NEG), writes=[tb])
        ts(P, 'dve', tb, tb[:, OFF:OFF + 128], pt_v, 8.0, None, ALU.mult, None, [pt])
        cp(P, 'dve', tbb, tbb[:], tb[:], [tb])
        P.dma('sp', K.ftab[0], tbb[:], reads=[tbb])
        P.op('dve', lambda e: e.memset(tb[:, OFF + 512:TABL], NEG), writes=[tb])
        cp(P, 'dve', tbb, tbb[:], tb[:], [tb])
        P.dma('sp', K.ftab[1], tbb[:], reads=[tbb])


def phase_nsa(K, G, l):
    P = K.P
    with Scope(K) as sc:
        pos32 = sc.sb([32, 128], F32); posT = sc.sb([64, 32, 2], BF16); pp = sc.ps([128, 32], F32)
        P.op('dve', lambda e: e.memset(pos32[:], 0.0), writes=[pos32])
        ph = sc.ps([64, 256], F32); pb = sc.ps([64, 2], F32); pk = sc.ps([128, 256], F32)
        Mf = sc.sb([128, 2, 64], F32)
        P.dma('sp', Mf[:], K.C['cmpM'].rearrange("(t p) c -> p t c", p=128), writes=[Mf])
        for j in range(2):
            w1 = sc.sb([64, 32, 64], BF16); w2 = sc.sb([64, 64], BF16)
            load_w_cast(P, w1, w1[:], K.W['nsa_cmp_w1'][l, j].rearrange("(l d) o -> d l o", d=64))
            load_w_cast(P, w2, w2[:], K.W['nsa_cmp_w2'][l, j])
            P.dma('sp', pos32[:, 0:64], K.W['nsa_cmp_pos'][l, j], writes=[pos32])
            tr(P, pp, pp[:], pos32[:], G.ident_f[0:32, 0:32], [pos32, G.ident_f])
            cp(P, 'act', posT, posT[:, :, 0], pp[0:64, :], [pp])
            cp(P, 'act', posT, posT[:, :, 1], pp[0:64, :], [pp])
            hb = sc.sb([64, 1], F32)
            for ll in range(32):
                mm(P, pb, pb[:], w1[:, ll, :], posT[:, ll, :], ll == 0, ll == 31, [w1, posT])
            cp(P, 'act', hb, hb[:], pb[:, 0:1], [pb])
            for g in range(2):
                kvc = sc.sb([64, S + 16], BF16); gl = sc.sb([64, 256], BF16)
                P.op('dve', lambda e, kvc=kvc: e.memset(kvc[:, S:S + 16], 0.0), writes=[kvc])
                P.dma('sp', kvc[:, 0:S], K.kvcT[j * 128 + g * 64:j * 128 + (g + 1) * 64, :], writes=[kvc])
                v3 = kvc[:, :].rearrange("p (n s) -> p n s", s=16)
                for ll in range(32):
                    mm(P, ph, ph[:, 0:256], w1[:, ll, :], v3[:, (ll // 16):(ll // 16) + 256, ll % 16], ll == 0, ll == 31, [w1, kvc])
                P.op('dve', lambda e, gl=gl: e.memset(gl[:], 0.0), writes=[gl])
                act(P, gl, gl[:, 0:255], ph[:, 0:255], AF.Gelu, [ph, hb], bias=hb[:])
                if j == 0:
                    kc = sc.sb([64, 256], BF16)
                    mm(P, pk, pk[0:64, :], w2[:], gl[:], True, True, [w2, gl])
                    cp(P, 'act', kc, kc[:], pk[0:64, :], [pk])
                    P.dma('sp', K.kcT[g], kc[:], reads=[kc])
                else:
                    va = sc.sb([128, 2, 129], BF16)
                    P.op('dve', lambda e, va=va: e.memset(va[:], 1.0), writes=[va])
                    cp(P, 'dve', va, va[:, :, 65:129], Mf[:], [Mf])
                    for nt in range(2):
                        mm(P, pk, pk[:, nt * 64:(nt + 1) * 64], gl[:, nt * 128:(nt + 1) * 128], w2[:], True, True, [w2, gl])
                    cp(P, 'act', va, va[:, :, 0:64], pk[:, 0:128].rearrange("p (t c) -> p t c", t=2), [pk])
                    P.dma('sp', K.vcA[g].rearrange("(t p) c -> p t c", p=128), va[:], reads=[va])
    with Scope(K) as sc:
        R = attn_tiles(sc, 64)
        ksT = [sc.sb([64, S], BF16) for _ in range(2)]; kwT = [sc.sb([64, S], BF16) for _ in range(2)]
        kcT = [sc.sb([64, 256], BF16) for _ in range(2)]
        vs = [sc.sb([128, NT, 65], BF16) for _ in range(2)]; vw = [sc.sb([128, NT, 65], BF16) for _ in range(2)]
        vcA = [sc.sb([128, 2, 129], BF16) for _ in range(2)]
        E = sc.sb([64, S], BF16)
        load_w_cast(P, E, E[:, 0:2048], K.C['expand'][:, 0:2048]); load_w_cast(P, E, E[:, 2048:S], K.C['expand'][:, 2048:S])
        for g in range(2):
            P.dma('sp', ksT[g][:], K.ksT[g * 64:(g + 1) * 64, :], writes=[ksT[g]])
            P.dma('sp', kwT[g][:], K.kwT[g * 64:(g + 1) * 64, :], writes=[kwT[g]])
            P.dma('sp', kcT[g][:], K.kcT[g], writes=[kcT[g]])
            for q4 in range(4):
                isl = slice(q4 * 8, (q4 + 1) * 8); tsl = slice(q4 * 1024, (q4 + 1) * 1024)
                P.dma('sp', vs[g][:, isl, :], K.vs[tsl, g, :].rearrange("(i p) c -> p i c", p=128), writes=[vs[g]])
                P.dma('sp', vw[g][:, isl, :], K.vw[tsl, g, :].rearrange("(i p) c -> p i c", p=128), writes=[vw[g]])
            P.dma('sp', vcA[g][:], K.vcA[g].rearrange("(t p) c -> p t c", p=128), writes=[vcA[g]])
        gt = [sc.sb([128, 4, 24], F32) for _ in range(2)]
        qts = [sc.sb([64, 512], BF16) for _ in range(4)]
        bias = [sc.sb([128, 512], BF16) for _ in range(3)]
        bcnt = [0]
        impacc = sc.sb([128, 4, 64], F32)
        yacc = [sc.sb([128, 4, 64], F32) for _ in range(4)]
        sA = [sc.sb([128, 64], F32) for _ in range(2)]; sB = [sc.sb([128, 64], F32) for _ in range(2)]
        score = sc.sb([128, 64], F32); sc2 = sc.sb([128, 64], F32); m8a = sc.sb([128, 8], F32); m8b = sc.sb([128, 8], F32)
        selm = sc.sb([128, 64], F32); negs = sc.sb([128, 64], BF16); negT = sc.sb([64, 512], BF16)
        ptb = sc.ps([64, 128], BF16); ybf = sc.sb([128, 64], BF16)
        rs = [sc.sb([128, 1], F32) for _ in range(2)]; s1 = [sc.sb([128, 1], F32) for _ in range(2)]
        yst = [sc.sb([64, 512], BF16) for _ in range(2)]
        fcnt = [0]
        ftt = K.ftab.tensor

        def bias_tile(tab, h, off, kstep):
            b_ = bias[bcnt[0] % 3]; bcnt[0] += 1
            src = bass.AP(ftt, (tab * 8 + h) * TABL + off - 127 * kstep, [[kstep, 128], [1, 512]])
            P.dma('sp', b_[:], src, writes=[b_])
            return b_

        def norm_scale(po, gcol, dv_sum, gtn):
            i = fcnt[0]; fcnt[0] += 1
            r_ = rs[i % 2]; s_ = s1[i % 2]
            ts(P, 'dve', r_, r_[:], po[:, dv_sum:dv_sum + 1], 1e-30, None, ALU.max, None, [po])
            P.op('dve', lambda e: e.reciprocal(out=r_[:], in_=r_[:]), reads=[r_], writes=[r_])
            tt(P, 'dve', s_, s_[:], r_[:], gcol, ALU.mult, [r_, gtn])
            return r_, s_

        for qb in range(8):
            g_ = gt[qb % 2]
            P.dma('sp', g_[:], K.gate[qb * 512:(qb + 1) * 512, :].rearrange("(j p) c -> p j c", p=128), writes=[g_])
            for g in range(2):
                P.barrier()
                P.op('dve', lambda e: e.memset(impacc[:], 0.0), writes=[impacc])
                for hh in range(4):
                    h = g * 4 + hh
                    qt = qts[hh]
                    P.dma('sp', qt[:], K.qaT[h * 64:(h + 1) * 64, qb * 512:(qb + 1) * 512], writes=[qt])
                    R['qt'] = [qt, qt]

                    def extra_c(qb_, kb, h=h):
                        b_ = bias_tile(0, h, qb_ * 512 - 16 * kb * 128 - 31 + OFF, 16)
                        return [(G.identR_b[:], b_[:], [G.identR_b, b_])]

                    def fin_c(qb_, j, po, h=h, hh=hh, g_=g_):
                        r_, s_ = norm_scale(po, g_[:, j, h * 3:h * 3 + 1], 64, g_)
                        ts(P, 'dve', yacc[hh], yacc[hh][:, j, :], po[:, 0:64], s_[:, 0:1], None, ALU.mult, None, [po, s_])
                        stt(P, impacc, impacc[:, j, :], po[:, 65:129], r_[:, 0:1], impacc[:, j, :], ALU.mult, ALU.add, [po, r_, impacc])
                    attention(K, G, sc, R, None, 64, kcT[g], vcA[g], 129, lambda qb_: [0, 1], lambda a, b, c: True,
                              extra_c, 0.125, fin_c, qbs=[qb])
                for j in range(4):
                    i = qb * 4 + j
                    a_ = sA[j % 2]; b_ = sB[j % 2]
                    P.dma('sp', a_[:], K.C['selA'][i], writes=[a_]); P.dma('sp', b_[:], K.C['selB'][i], writes=[b_])
                    tt(P, 'dve', score, score[:], impacc[:, j, :], a_[:], ALU.mult, [impacc, a_])
                    tt(P, 'dve', score, score[:], score[:], b_[:], ALU.add, [score, b_])
                    P.op('dve', lambda e: e.max(out=m8a[:], in_=score[:]), reads=[score], writes=[m8a])
                    P.op('dve', lambda e: e.match_replace(out=sc2[:], in_to_replace=m8a[:], in_values=score[:], imm_value=-1e30),
                         reads=[m8a, score], writes=[sc2])
                    P.op('dve', lambda e: e.max(out=m8b[:], in_=sc2[:]), reads=[sc2], writes=[m8b])
                    ts(P, 'dve', selm, selm[:], score[:], m8b[:, 7:8], None, ALU.is_ge, None, [score, m8b])
                    ts(P, 'dve', negs, negs[:], selm[:], -1.0, -NEG, ALU.add, ALU.mult, [selm])
                    tr(P, ptb, ptb[:], negs[:], G.ident_b[:], [negs, G.ident_b])
                    cp(P, 'act', negT, negT[:, j * 128:(j + 1) * 128], ptb[:], [ptb])
                for hh in range(4):
                    h = g * 4 + hh
                    qt = qts[hh]
                    R['qt'] = [qt, qt]

                    def extra_s(qb_, kb, h=h):
                        ex = []
                        if kb >= qb_ * 4 - 1:
                            b_ = bias_tile(0, h, qb_ * 512 - kb * 128 + OFF, 1)
                            ex.append((G.identR_b[:], b_[:], [G.identR_b, b_]))
                        ex.append((E[:, kb * 128:(kb + 1) * 128], negT[:], [E, negT]))
                        return ex

                    def fin_s(qb_, j, po, h=h, hh=hh, g_=g_):
                        r_, s_ = norm_scale(po, g_[:, j, h * 3 + 1:h * 3 + 2], 64, g_)
                        stt(P, yacc[hh], yacc[hh][:, j, :], po[:, 0:64], s_[:, 0:1], yacc[hh][:, j, :], ALU.mult, ALU.add, [po, s_, yacc[hh]])
                    attention(K, G, sc, R, None, 64, ksT[g], vs[g], 65, lambda qb_: list(range(0, qb_ * 4 + 4)),
                              lambda qb_, kb, j: kb <= qb_ * 4 + j, extra_s, 0.125, fin_s, qbs=[qb])

                    def extra_w(qb_, kb, h=h):
                        b_ = bias_tile(1, h, qb_ * 512 - kb * 128 + OFF, 1)
                        return [(G.identR_b[:], b_[:], [G.identR_b, b_])]

                    def fin_w(qb_, j, po, h=h, hh=hh, g_=g_):
                        r_, s_ = norm_scale(po, g_[:, j, h * 3 + 2:h * 3 + 3], 64, g_)
                        stt(P, yacc[hh], yacc[hh][:, j, :], po[:, 0:64], s_[:, 0:1], yacc[hh][:, j, :], ALU.mult, ALU.add, [po, s_, yacc[hh]])
                        y_ = yst[h % 2]
                        cp(P, 'dve', ybf, ybf[:], yacc[hh][:, j, :], [yacc[hh]])
                        tr(P, ptb, ptb[:], ybf[:], G.ident_b[:], [ybf, G.ident_b])
                        cp(P, 'act', y_, y_[:, j * 128:(j + 1) * 128], ptb[:], [ptb])
                        if j == 3:
                            P.dma('sp', K.yaT[h * 64:(h + 1) * 64, qb_ * 512:(qb_ + 1) * 512], y_[:], reads=[y_])
                    attention(K, G, sc, R, None, 64, kwT[g], vw[g], 65, lambda qb_: list(range(max(0, qb_ * 4 - 4), qb_ * 4 + 4)),
                              lambda qb_, kb, j: (kb <= qb_ * 4 + j) and (kb >= qb_ * 4 + j - 4), extra_w, 0.125, fin_w, qbs=[qb])


D = 1024; S = 4096; L = 4; NT = S // 128
IN_COLS = 6584; DFF = 2816
A_COLS = 1304; B0 = 1304; C0 = 1720; G0 = 3512
OFF = 4224; TABL = OFF + 4096 + 128
NEG = -30000.0


def t5_bucket_np(d):
    d = np.maximum(d, 0)
    large = 16 + (np.log(np.maximum(d, 1).astype(np.float32) / 16) / math.log(128 / 16) * 16).astype(np.int32)
    return np.where(d < 16, d, np.minimum(large, 31))


def host_consts():
    c = {}
    c['ident'] = np.eye(128, dtype=np.float32)
    c['identR'] = np.ascontiguousarray(np.eye(128, dtype=np.float32)[::-1])
    su = np.triu(np.ones((64, 64), np.float32), 1); ui = np.triu(np.ones((64, 64), np.float32), 0)
    c['rmask'] = np.stack([np.tile(su, (1, 8)), np.tile(ui, (1, 8)), np.tile(su.T, (1, 8)),
                           np.tile(np.eye(64, dtype=np.float32), (1, 8))], 0)
    rs = np.ones((64, 512), np.float32); rs[:, ::64] = 0.0
    c['rreset'] = rs
    half = 16
    inv = 10000.0 ** (-np.arange(half, dtype=np.float32) / half)
    ang = np.arange(S, dtype=np.float32)[None, :] * inv[:, None]
    cos, sin = np.cos(ang), np.sin(ang)
    c['rope'] = np.stack([np.concatenate([cos, cos], 0), np.concatenate([-sin, sin], 0)], 0).astype(np.float32)
    cm = np.zeros((4, 128, 512), np.float32)
    for o in range(4):
        k = o * 128 + np.arange(128)[:, None]; q = np.arange(512)[None, :]
        cm[o] = np.where(q >= k, 0.0, NEG)
    c['cmask'] = cm
    dl = np.arange(128)
    oh = (t5_bucket_np(dl)[None, :] == np.arange(32)[:, None]).astype(np.float32)
    oh[31, :] -= 1.0
    c['t5oh'] = oh
    n_cmp = 255
    M = np.zeros((256, 64), np.float32)
    for j in range(64):
        for a in range(4):
            for cc in range(2):
                n = 4 * j - a - cc
                if 0 <= n < n_cmp:
                    M[n, j] += 1.0
    c['cmpM'] = M
    E = np.zeros((64, S), np.float32)
    E[np.arange(S) // 64, np.arange(S)] = 1.0
    c['expand'] = E
    selA = np.zeros((NT, 128, 64), np.float32); selB = np.zeros((NT, 128, 64), np.float32)
    for i in range(NT):
        t = i * 128 + np.arange(128); cur = (t // 64)[:, None]; j = np.arange(64)[None, :]
        back = cur - j
        forced = (j == 0) | ((back >= 0) & (back < 2))
        selA[i] = ((back >= 0) & ~forced).astype(np.float32)
        selB[i] = np.where(forced, 1e9, np.where(back >= 0, 0.0, -1.0))
    c['selA'] = selA; c['selB'] = selB
    return c


class Ctx:
    pass


def build(debug_out=(), stages=None, nlayers=L, dbg_in=(), LW=L, wnames=None, cnames=None):
    nc = bass.Bass("TRN2", target_bir_lowering=False)
    K = Ctx(); K.nc = nc
    din = lambda n, s: nc.dram_tensor(n, list(s), F32, kind="ExternalInput").ap()
    K.x_in = din("x", [S, D]); K.c_in = din("c", [1, D]); K.rel_bias = din("rel_bias", [32, 8])
    shp = dict(ada_w=[LW, D, 6 * D], ada_b=[LW, 6 * D], norm_gain=[LW, 4, D], w_in=[LW, D, IN_COLS],
               nsa_cmp_pos=[LW, 2, 32, 64], nsa_cmp_w1=[LW, 2, 2048, 64], nsa_cmp_w2=[LW, 2, 64, 64],
               mla_q_norm=[LW, 256], mla_kv_norm=[LW, 128], mla_w_uq=[LW, 256, 768], mla_w_ukv=[LW, 128, 1024],
               rwkv_mu=[LW, 1792], rwkv_w0=[LW, 512], rwkv_a0=[LW, 512], rwkv_k_k=[LW, 512], rwkv_k_a=[LW, 512],
               rwkv_w2=[LW, 64, 512], rwkv_a2=[LW, 64, 512], rwkv_g2=[LW, 128, 512], rwkv_r_k=[LW, 8, 64],
               rwkv_ln=[LW, 2, 512], w_branch=[LW, 1536, D], w_out=[LW, D, D], ffn_up=[LW, D, 2 * DFF],
               ffn_conv_w=[LW, 3, 2 * DFF], ffn_conv_b=[LW, 2 * DFF], ffn_down=[LW, DFF, D])
    K.W = {k: din(k, v) for k, v in shp.items() if wnames is None or k in wnames}
    hc = host_consts()
    K.C = {k: din("cst_" + k, v.shape) for k, v in hc.items() if cnames is None or k in cnames}
    K.in_names = set(['x', 'c', 'rel_bias']) | set(K.W) | set('cst_' + k for k in K.C)
    K.out = nc.dram_tensor("out", [S, D], F32, kind="ExternalOutput").ap()

    def scr(n, s, dt=F32):
        kind = "ExternalOutput" if n in debug_out else ("ExternalInput" if n in dbg_in else "Internal")
        return nc.dram_tensor(n, list(s), dt, kind=kind).ap()
    K.scr = scr
    K.xa = scr("xa", [S, D]); K.xb = scr("xb", [S, D])
    K.qaT = scr("qaT", [512, S], BF16); K.kvcT = scr("kvcT", [256, S], BF16)
    K.ksT = scr("ksT", [128, S], BF16); K.kwT = scr("kwT", [128, S], BF16)
    K.vs = scr("vs", [S, 2, 65], BF16); K.vw = scr("vw", [S, 2, 65], BF16)
    K.gate = scr("gate", [S, 24]); K.zbT = scr("zbT", [448, S]); K.zcT = scr("zcT", [1792, S + 1])
    K.mqT = scr("mqT", [8, 96, S], BF16); K.mkT = scr("mkT", [8, 96, S], BF16); K.mv = scr("mv", [S, 8, 65], BF16)
    K.yaT = scr("yaT", [512, S], BF16); K.ybT = scr("ybT", [512, S], BF16); K.ycT = scr("ycT", [512, S], BF16)
    K.hTd = scr("hTd", [D, S], BF16); K.gT = scr("gT", [DFF, S], BF16)
    K.ftab = scr("ftab", [2, 8, TABL], BF16)
    K.kcT = scr("kcT", [2, 64, 256], BF16); K.vcA = scr("vcA", [2, 256, 129], BF16)

    with ExitStack() as gst:
        P = Prog(nc, gst); K.P = P
        run_all(K, gst, stages, nlayers)
        P.finish()
        P.emit()
    K.hc = hc
    return nc, K


def make_inmaps(inputs, hc, extra=None, layer=None, names=None):
    maps = []
    shared = {}
    for k, v in inputs.items():
        if k not in ('x', 'c'):
            if layer is not None and k != 'rel_bias':
                v = v[layer:layer + 1]
            shared[k] = np.ascontiguousarray(v)
    for b in range(8):
        m = {"x": np.ascontiguousarray(inputs['x'][b]), "c": np.ascontiguousarray(inputs['c'][b:b + 1])}
        m.update(shared)
        for k, v in hc.items():
            m["cst_" + k] = v
        if extra:
            m.update(extra[b] if isinstance(extra, list) else extra)
        if names is not None:
            m = {k: v for k, v in m.items() if k in names or (extra and k in (extra[b] if isinstance(extra, list) else extra))}
        maps.append(m)
    return maps


def kernel(**inputs):
    inputs = {k: np.asarray(v) for k, v in inputs.items()}
    nc, K = build(nlayers=1, LW=1)
    cur = dict(inputs)
    for l in range(L):
        maps = make_inmaps(cur, K.hc, layer=l)
        res = run_bass_kernel_spmd(nc, maps, core_ids=list(range(8)))
        x = np.stack([np.asarray(r["out"], dtype=np.float32) for r in res.results], 0)
        cur['x'] = x
    return cur['x']
```
